# Optimizing a Trainium2 kernel written in Bass

```python
import math
import jax, jax.numpy as jnp
from jax import lax
import numpy as np

D_MODEL = 1024
BATCH = 16
SEQ = 2048
DEPTH = 4

GRID_W = 64
CTX_LEN = 256
D_S5 = 512
S5_GROUP = 16
S5_GROUPS = D_S5 // S5_GROUP
S5_STATE = 64
D_CONV = 512
CONV_K = 31
D_FF = 2816
FFN_K = 3
N_BRANCH = 2
OFF_CONV = D_S5
OFF_GATE = D_S5 + 2 * D_CONV
D_IN = OFF_GATE + N_BRANCH * D_MODEL
N_MOD = 6
EPS = 1e-6
DT_MIN = 1e-3
DT_MAX = 1e-1

kernel_name = "hybrid_s5_conformer_prefix_dit"


def rmsnorm(x, g):
    xf = x.astype(jnp.float32)
    y = xf * lax.rsqrt(jnp.mean(xf * xf, axis=-1, keepdims=True) + EPS)
    return (y * g.astype(jnp.float32)).astype(x.dtype)


def layernorm(x, g, b):
    xf = x.astype(jnp.float32)
    mu = jnp.mean(xf, axis=-1, keepdims=True)
    var = jnp.mean(jnp.square(xf - mu), axis=-1, keepdims=True)
    y = (xf - mu) * lax.rsqrt(var + EPS)
    return (y * g.astype(jnp.float32) + b.astype(jnp.float32)).astype(x.dtype)


def modulate(h, shift, scale):
    return h * (1.0 + scale) + shift


def depthwise_conv1d(x, w, b):
    k = w.shape[0]
    y = lax.conv_general_dilated(x, w[:, None, :].astype(x.dtype), window_strides=(1,),
                                 padding=[(k // 2, k // 2)],
                                 dimension_numbers=("NWC", "WIO", "NWC"),
                                 feature_group_count=x.shape[-1])
    return y + b


def depthwise_conv2d(x, w, b):
    y = lax.conv_general_dilated(x, w[:, :, None, :].astype(x.dtype), window_strides=(1, 1),
                                 padding="SAME",
                                 dimension_numbers=("NHWC", "HWIO", "NHWC"),
                                 feature_group_count=x.shape[-1])
    return y + b


def s5_discretise(lam_re, lam_im, log_dt, b_re, b_im):
    dt = jnp.exp(log_dt.astype(jnp.float32))[:, None]
    lr = lam_re.astype(jnp.float32)
    li = lam_im.astype(jnp.float32)
    mag = jnp.exp(dt * lr)
    ab_re = mag * jnp.cos(dt * li)
    ab_im = mag * jnp.sin(dt * li)
    den = lr * lr + li * li
    nr = ab_re - 1.0
    k_re = (nr * lr + ab_im * li) / den
    k_im = (ab_im * lr - nr * li) / den
    br = b_re.astype(jnp.float32)
    bi = b_im.astype(jnp.float32)
    bb_re = k_re[..., None] * br - k_im[..., None] * bi
    bb_im = k_re[..., None] * bi + k_im[..., None] * br
    return ab_re, ab_im, bb_re, bb_im


def _s5_combine(e1, e2):
    a1r, a1i, b1r, b1i = e1
    a2r, a2i, b2r, b2i = e2
    return (a1r * a2r - a1i * a2i,
            a1r * a2i + a1i * a2r,
            a2r * b1r - a2i * b1i + b2r,
            a2r * b1i + a2i * b1r + b2i)


def s5_scan(ab_re, ab_im, bb_re, bb_im, u, h0, reverse):
    uf = u.astype(jnp.float32)
    bu_re = jnp.einsum("blgh,gph->blgp", uf, bb_re)
    bu_im = jnp.einsum("blgh,gph->blgp", uf, bb_im)
    if h0 is not None:
        idx = -1 if reverse else 0
        h0r, h0i = h0
        bu_re = bu_re.at[:, idx].add(ab_re * h0r - ab_im * h0i)
        bu_im = bu_im.at[:, idx].add(ab_re * h0i + ab_im * h0r)
    seq = u.shape[1]
    a_re = jnp.broadcast_to(ab_re, (1, seq) + ab_re.shape)
    a_im = jnp.broadcast_to(ab_im, (1, seq) + ab_im.shape)
    _, _, hr, hi = lax.associative_scan(_s5_combine, (a_re, a_im, bu_re, bu_im),
                                        reverse=reverse, axis=1)
    return hr, hi


def s5_readout(hr, hi, c_re, c_im):
    y = (jnp.einsum("blgp,ghp->blgh", hr, c_re.astype(jnp.float32))
         - jnp.einsum("blgp,ghp->blgh", hi, c_im.astype(jnp.float32)))
    return y.reshape(y.shape[0], y.shape[1], D_S5)


def s5_mixer(u_lat, u_ctx, lam_re, lam_im, log_dt, b_re, b_im, c_re, c_im, d, want_ctx):
    bsz, seq, _ = u_lat.shape
    ul = u_lat.reshape(bsz, seq, S5_GROUPS, S5_GROUP)
    uc = u_ctx.reshape(bsz, u_ctx.shape[1], S5_GROUPS, S5_GROUP)
    dd = d.astype(jnp.float32)
    y_lat = dd * u_lat.astype(jnp.float32)
    y_ctx = dd * u_ctx.astype(jnp.float32) if want_ctx else None
    for direction, reverse in enumerate((False, True)):
        ab_re, ab_im, bb_re, bb_im = s5_discretise(lam_re[direction], lam_im[direction],
                                                   log_dt[direction], b_re[direction], b_im[direction])
        hc_r, hc_i = s5_scan(ab_re, ab_im, bb_re, bb_im, uc, None, reverse)
        edge = 0 if reverse else -1
        h0 = (hc_r[:, edge], hc_i[:, edge])
        hl_r, hl_i = s5_scan(ab_re, ab_im, bb_re, bb_im, ul, h0, reverse)
        y_lat = y_lat + s5_readout(hl_r, hl_i, c_re[direction], c_im[direction])
        if want_ctx:
            y_ctx = y_ctx + s5_readout(hc_r, hc_i, c_re[direction], c_im[direction])
    y_lat = y_lat.astype(u_lat.dtype)
    if want_ctx:
        y_ctx = y_ctx.astype(u_ctx.dtype)
    return y_lat, y_ctx


def s5_glu_out(y, w_glu, w_a):
    g = jax.nn.gelu(y)
    return (g * jax.nn.sigmoid(g @ w_glu)) @ w_a


def conformer_branch(v, dw, dw_b, ln_g, ln_b, w_b):
    a = v[..., :D_CONV] * jax.nn.sigmoid(v[..., D_CONV:])
    a = depthwise_conv1d(a, dw, dw_b)
    a = layernorm(a, ln_g, ln_b)
    return jax.nn.silu(a) @ w_b


def gated_merge(z, ya, yb, w_out):
    ga = z[..., OFF_GATE:OFF_GATE + D_MODEL]
    gb = z[..., OFF_GATE + D_MODEL:OFF_GATE + 2 * D_MODEL]
    return (jax.nn.sigmoid(ga) * ya + jax.nn.sigmoid(gb) * yb) @ w_out


def conv_ffn(h, w_up, dw, dw_b, w_down, rows):
    up = h @ w_up
    gate, val = up[..., :D_FF], up[..., D_FF:]
    bsz, seq, _ = h.shape
    if rows > 0:
        gate = depthwise_conv2d(gate.reshape(bsz, rows, GRID_W, D_FF), dw, dw_b).reshape(bsz, seq, D_FF)
    else:
        gate = depthwise_conv1d(gate, dw[FFN_K // 2], dw_b)
    return (jax.nn.silu(gate) * val) @ w_down


def setup_inputs(seed: int = 0) -> dict:
    key = jax.random.key(seed)
    ks = list(jax.random.split(key, 32))

    def nrm(shape, s):
        return jax.random.normal(ks.pop(), shape, jnp.float32) * s

    L = DEPTH
    G, P, H = S5_GROUPS, S5_STATE, S5_GROUP
    n_idx = jnp.arange(P, dtype=jnp.float32)
    inp = {}
    inp["x"] = nrm((BATCH, SEQ, D_MODEL), 1.0)
    inp["c"] = nrm((BATCH, D_MODEL), 1.0)
    inp["ctx"] = nrm((BATCH, CTX_LEN, D_MODEL), 1.0)
    inp["c_ctx"] = nrm((D_MODEL,), 1.0)
    inp["ada_w"] = nrm((L, D_MODEL, N_MOD * D_MODEL), 0.5 * D_MODEL ** -0.5)
    inp["ada_b"] = nrm((L, N_MOD * D_MODEL), 0.02)
    inp["norm1_g"] = 1.0 + nrm((L, D_MODEL), 0.02)
    inp["w_in"] = nrm((L, D_MODEL, D_IN), D_MODEL ** -0.5)
    inp["s5_lam_re"] = -0.5 + nrm((L, 2, G, P), 0.01)
    inp["s5_lam_im"] = jnp.pi * n_idx + nrm((L, 2, G, P), 0.01)
    inp["s5_log_dt"] = jax.random.uniform(ks.pop(), (L, 2, G), jnp.float32,
                                          math.log(DT_MIN), math.log(DT_MAX))
    inp["s5_b_re"] = nrm((L, 2, G, P, H), (2 * H) ** -0.5)
    inp["s5_b_im"] = nrm((L, 2, G, P, H), (2 * H) ** -0.5)
    inp["s5_c_re"] = nrm((L, 2, G, H, P), P ** -0.5)
    inp["s5_c_im"] = nrm((L, 2, G, H, P), P ** -0.5)
    inp["s5_d"] = nrm((L, D_S5), 1.0)
    inp["w_glu"] = nrm((L, D_S5, D_S5), D_S5 ** -0.5)
    inp["w_a"] = nrm((L, D_S5, D_MODEL), D_S5 ** -0.5)
    inp["conv_dw"] = nrm((L, CONV_K, D_CONV), CONV_K ** -0.5)
    inp["conv_dw_b"] = nrm((L, D_CONV), 0.01)
    inp["conv_ln_g"] = 1.0 + nrm((L, D_CONV), 0.02)
    inp["conv_ln_b"] = nrm((L, D_CONV), 0.01)
    inp["w_b"] = nrm((L, D_CONV, D_MODEL), D_CONV ** -0.5)
    inp["w_out"] = nrm((L, D_MODEL, D_MODEL), D_MODEL ** -0.5)
    inp["norm2_g"] = 1.0 + nrm((L, D_MODEL), 0.02)
    inp["ffn_w_up"] = nrm((L, D_MODEL, 2 * D_FF), D_MODEL ** -0.5)
    inp["ffn_dw"] = nrm((L, FFN_K, FFN_K, D_FF), 1.0 / FFN_K)
    inp["ffn_dw_b"] = nrm((L, D_FF), 0.01)
    inp["ffn_w_down"] = nrm((L, D_FF, D_MODEL), D_FF ** -0.5)
    inp["final_g"] = 1.0 + nrm((D_MODEL,), 0.02)
    return inp


def reference(x, c, ctx, c_ctx, ada_w, ada_b, norm1_g, w_in, s5_lam_re, s5_lam_im, s5_log_dt,
              s5_b_re, s5_b_im, s5_c_re, s5_c_im, s5_d, w_glu, w_a, conv_dw, conv_dw_b,
              conv_ln_g, conv_ln_b, w_b, w_out, norm2_g, ffn_w_up, ffn_dw, ffn_dw_b,
              ffn_w_down, final_g):
    rows = x.shape[1] // GRID_W
    silu_c = jax.nn.silu(c)
    silu_cc = jax.nn.silu(c_ctx)
    h_lat = x
    h_ctx = ctx
    for i in range(DEPTH):
        last = i == DEPTH - 1
        mod_l = (silu_c @ ada_w[i] + ada_b[i])[:, None, :]
        sh1, sc1, g1, sh2, sc2, g2 = jnp.split(mod_l, N_MOD, axis=-1)
        mod_c = silu_cc @ ada_w[i] + ada_b[i]
        csh1, csc1, cg1, csh2, csc2, cg2 = jnp.split(mod_c, N_MOD, axis=-1)

        nl = modulate(rmsnorm(h_lat, norm1_g[i]), sh1, sc1)
        nc = modulate(rmsnorm(h_ctx, norm1_g[i]), csh1, csc1)
        zl = nl @ w_in[i]
        zc = nc @ (w_in[i][:, :D_S5] if last else w_in[i])
        ys_l, ys_c = s5_mixer(zl[..., :D_S5], zc[..., :D_S5], s5_lam_re[i], s5_lam_im[i],
                              s5_log_dt[i], s5_b_re[i], s5_b_im[i], s5_c_re[i], s5_c_im[i],
                              s5_d[i], not last)
        ya_l = s5_glu_out(ys_l, w_glu[i], w_a[i])
        yb_l = conformer_branch(zl[..., OFF_CONV:OFF_GATE], conv_dw[i], conv_dw_b[i],
                                conv_ln_g[i], conv_ln_b[i], w_b[i])
        h_lat = h_lat + g1 * gated_merge(zl, ya_l, yb_l, w_out[i])
        if not last:
            ya_c = s5_glu_out(ys_c, w_glu[i], w_a[i])
            yb_c = conformer_branch(zc[..., OFF_CONV:OFF_GATE], conv_dw[i], conv_dw_b[i],
                                    conv_ln_g[i], conv_ln_b[i], w_b[i])
            h_ctx = h_ctx + cg1 * gated_merge(zc, ya_c, yb_c, w_out[i])

        nl = modulate(rmsnorm(h_lat, norm2_g[i]), sh2, sc2)
        h_lat = h_lat + g2 * conv_ffn(nl, ffn_w_up[i], ffn_dw[i], ffn_dw_b[i], ffn_w_down[i], rows)
        if not last:
            nc = modulate(rmsnorm(h_ctx, norm2_g[i]), csh2, csc2)
            h_ctx = h_ctx + cg2 * conv_ffn(nc, ffn_w_up[i], ffn_dw[i], ffn_dw_b[i], ffn_w_down[i], 0)
    return rmsnorm(h_lat, final_g)
```

```python
import contextlib
import math
import numpy as np
import concourse.bass as bass
import concourse.mybir as mybir
from concourse.bass_utils import run_bass_kernel_spmd

F32 = mybir.dt.float32
BF16 = mybir.dt.bfloat16
I32 = mybir.dt.int32
AF = mybir.ActivationFunctionType
ALU = mybir.AluOpType

D = 1024
KT = 8
LCTX = 256
LLAT = 2048
LT = LCTX + LLAT
NQ = LT // 8
QC = LCTX // 8
DEPTH = 4
NH = 22
EPS = 1e-6
TILES = [(0, 256, True), (256, 512, False), (768, 512, False), (1280, 512, False), (1792, 512, False)]
APAD = 2352
GCTX = 258
GLAT = 34 * 66
TWO_PI = 2.0 * math.pi


def apos(t0):
    return t0 + 16 if t0 < LCTX else t0 + 32


class Tracker:
    def __init__(self, nc, es, ndsem=48):
        self.nc = nc
        self.eng = {"pe": nc.tensor, "act": nc.scalar, "dve": nc.vector, "pool": nc.gpsimd, "sp": nc.sync}
        self.sem = {e: es.enter_context(nc.semaphore("sem_" + e)) for e in self.eng}
        self.cnt = {e: 0 for e in self.eng}
        self.dsem = [es.enter_context(nc.semaphore("dsem%d" % i)) for i in range(ndsem)]
        self.dcnt = [0] * ndsem
        self.dpool = {"sp": list(range(0, 20)), "pool": list(range(20, ndsem))}
        self.dnext = {"sp": 0, "pool": 0}
        self.waited = {e: {} for e in self.eng}
        self.res = {}

    def _semh(self, key):
        return self.sem[key[1]] if key[0] == "e" else self.dsem[key[1]]

    def _collect(self, e, reads, writes, is_dma):
        need = {}

        def add(dep, kind):
            semkey, val, src = dep
            if (not is_dma) and src == e and (e == "pe" or kind != "raw"):
                return
            if need.get(semkey, 0) < val:
                need[semkey] = val

        for k in reads:
            r = self.res.get(k)
            if r is not None and r["w"] is not None:
                add(r["w"], "raw")
        for k in writes:
            r = self.res.get(k)
            if r is not None:
                if r["w"] is not None:
                    add(r["w"], "waw")
                for dep in r["r"].values():
                    add(dep, "war")
        return need

    def _wait(self, e, need):
        for semkey, val in need.items():
            if self.waited[e].get(semkey, 0) < val:
                self.eng[e].wait_ge(self._semh(semkey), val)
                self.waited[e][semkey] = val

    def _record(self, dep, reads, writes):
        for k in writes:
            self.res[k] = {"w": dep, "r": {}}
        for k in reads:
            r = self.res.setdefault(k, {"w": None, "r": {}})
            r["r"][dep[0]] = dep

    def op(self, e, reads, writes, fn):
        self._wait(e, self._collect(e, reads, writes, False))
        inst = fn()
        self.cnt[e] += 1
        inst.then_inc(self.sem[e], 1)
        self._record((("e", e), self.cnt[e], e), reads, writes)

    def dma(self, q, out, in_, reads, writes):
        need = self._collect(q, reads, writes, True)
        pool_ = self.dpool[q]
        s = pool_[self.dnext[q]]
        self.dnext[q] = (self.dnext[q] + 1) % len(pool_)
        if self.dcnt[s] > 0:
            need[("d", s)] = max(need.get(("d", s), 0), self.dcnt[s])
        self._wait(q, need)
        inst = self.eng[q].dma_start(out=out, in_=in_)
        self.dcnt[s] += 16
        inst.then_inc(self.dsem[s], 16)
        self._record((("d", s), self.dcnt[s], "dma"), reads, writes)

    def barrier(self):
        for e in self.eng:
            need = {}
            for e2 in self.eng:
                if e2 != e and self.cnt[e2] > 0:
                    need[("e", e2)] = self.cnt[e2]
            for s in range(len(self.dsem)):
                if self.dcnt[s] > 0:
                    need[("d", s)] = self.dcnt[s]
            self._wait(e, need)
        self.res = {}

    def finish(self, e="sp"):
        need = {}
        for e2 in self.eng:
            if e2 != e and self.cnt[e2] > 0:
                need[("e", e2)] = self.cnt[e2]
        for s in range(len(self.dsem)):
            if self.dcnt[s] > 0:
                need[("d", s)] = self.dcnt[s]
        self._wait(e, need)


def build_program(layers, nseq=2, fused=True, debug=False, prof=False):
    nc = bass.Bass("TRN2", target_bir_lowering=False)
    last_layer = DEPTH - 1
    final = last_layer in layers

    def din(name, shape, dt=F32):
        return nc.dram_tensor(name, list(shape), dt, kind="ExternalInput").ap()

    xT = din("xT", [nseq, 128, KT, LT])
    cvec = din("cvec", [128, KT, 3])
    ada_w = din("ada_w", [DEPTH, 128, KT, 6 * D])
    ada_b = din("ada_b", [DEPTH, 128, 48])
    n1g = din("n1g", [128, DEPTH, KT])
    n2g = din("n2g", [128, DEPTH, KT])
    fing = din("fing", [128, KT])
    w_inA = din("w_inA", [DEPTH, 12, 128, KT, 128])
    w_inG = din("w_inG", [DEPTH, 16, 128, KT, 128])
    w_glu = din("w_glu", [DEPTH, 128, 4, 512])
    w_a = din("w_a", [DEPTH, 8, 128, 4, 128])
    w_b = din("w_b", [DEPTH, 8, 128, 4, 128])
    w_out = din("w_out", [DEPTH, 8, 128, KT, 128])
    w_up = din("w_up", [DEPTH, NH, 128, KT, 256])
    w_dn = din("w_dn", [DEPTH, 128, NH, D])
    cdw = din("cdw", [128, DEPTH, 4, 31])
    cdb = din("cdb", [128, DEPTH, 4])
    clg = din("clg", [128, DEPTH, 4])
    clb = din("clb", [128, DEPTH, 4])
    fdw = din("fdw", [128, DEPTH, NH, 9])
    fdb = din("fdb", [128, DEPTH, NH])
    lamre = din("lamre", [DEPTH, 128, 32])
    lamim = din("lamim", [DEPTH, 128, 32])
    logdt = din("logdt", [DEPTH, 128, 32])
    bre = din("bre", [DEPTH, 128, 32, 16])
    bim = din("bim", [DEPTH, 128, 32, 16])
    cre = din("cre", [DEPTH, 128, 32, 16])
    cim = din("cim", [DEPTH, 128, 32, 16])
    s5d = din("s5d", [DEPTH, 128, 32])
    ident_d = din("ident", [128, 128])
    sel1_d = din("sel1", [128, 64, 128])
    sel2_d = din("sel2", [128, 64, 128])
    evals_d = din("evals", [128, 8, 32])
    negmask_d = din("negmask", [128, 2, 128])
    jvals_d = din("jvals", [128, 16, 32])

    if final:
        out_d = nc.dram_tensor("out", [nseq, 128, KT, LLAT], F32, kind="ExternalOutput").ap()
    if fused:
        hbuf = nc.dram_tensor("hbuf", [nseq, 128, KT, LT], F32).ap()
    else:
        hbuf = nc.dram_tensor("hout", [nseq, 128, KT, LT], F32, kind="ExternalOutput").ap()
    nlbuf = nc.dram_tensor("nlbuf", [128, KT, LT], BF16).ap()
    nlbuf2 = nc.dram_tensor("nlbuf2", [128, KT, LT], BF16).ap()
    dbg = {}

    def dbg_out(name, shape, dt=F32):
        dbg[name] = nc.dram_tensor("dbg_" + name, list(shape), dt, kind="ExternalOutput").ap()
        return dbg[name]

    es = contextlib.ExitStack()
    with es:
        tr = Tracker(nc, es)
        E = es.enter_context

        uid = [0]

        def scope(name, stack):
            if prof:
                stack.enter_context(nc.named_scope(name))

        def sb(name, shape, dt, stack=None):
            uid[0] += 1
            return (stack or es).enter_context(nc.sbuf_tensor("s%d_%s" % (uid[0], name), list(shape), dt))

        ps = [E(nc.psum_tensor("ps%d" % i, [128, 512], F32)) for i in range(7)]
        psb = E(nc.psum_tensor("psb", [128, 1024], BF16))
        ps_i = [0]

        def bank():
            i = ps_i[0]
            ps_i[0] = (i + 1) % 7
            return i

        identf = sb("identf", [128, 128], F32)
        identb = sb("identb", [128, 128], BF16)
        onesb = sb("onesb", [128, 128], BF16)
        o512b = sb("o512b", [128, 128], BF16)
        epsc = sb("epsc", [128, 1], F32)
        negmask = sb("negmask", [128, 2, 128], F32)
        evals = sb("evals", [128, 8, 32], F32)
        n1g_s = sb("n1g_s", [128, DEPTH, KT], F32)
        n2g_s = sb("n2g_s", [128, DEPTH, KT], F32)
        fing_s = sb("fing_s", [128, KT], F32)
        cdw_s = sb("cdw_s", [128, DEPTH, 4, 31], F32)
        cdb_s = sb("cdb_s", [128, DEPTH, 4], F32)
        clg_s = sb("clg_s", [128, DEPTH, 4], F32)
        clb_s = sb("clb_s", [128, DEPTH, 4], F32)
        fdw_s = sb("fdw_s", [128, DEPTH, NH, 9], F32)
        fdb_s = sb("fdb_s", [128, DEPTH, NH], F32)
        mods = {l: sb("mods%d" % l, [128, 48, 3], F32) for l in layers}
        ms1 = {l: sb("ms1_%d" % l, [128, KT, 3], F32) for l in layers}
        ms2 = {l: sb("ms2_%d" % l, [128, KT, 3], F32) for l in layers}

        for (dst, src, key) in [(identf, ident_d, "identf"), (negmask, negmask_d, "negmask"), (evals, evals_d, "evals"),
                                (n1g_s, n1g, "n1g"), (n2g_s, n2g, "n2g"), (fing_s, fing, "fing"), (cdw_s, cdw, "cdw"),
                                (cdb_s, cdb, "cdb"), (clg_s, clg, "clg"), (clb_s, clb, "clb"), (fdw_s, fdw, "fdw"),
                                (fdb_s, fdb, "fdb")]:
            tr.dma("sp", dst[:], src, [], [key])
        tr.dma("pool", identb[:], ident_d, [], ["identb"])
        tr.op("dve", [], ["onesb"], lambda: nc.vector.memset(onesb[:], 1.0))
        tr.op("dve", [], ["o512b"], lambda: nc.vector.memset(o512b[:], 1.0 / 512.0))
        tr.op("dve", [], ["epsc"], lambda: nc.vector.memset(epsc[:], EPS))

        def phase_mods():
            with contextlib.ExitStack() as st:
                cv = sb("cv", [128, KT, 3], F32, st)
                scb = sb("scb", [128, KT, 3], BF16, st)
                adab = sb("adab", [128, 48], F32, st)
                wch = [sb("wch%d" % i, [128, KT, 1024], BF16, st) for i in range(2)]
                tr.dma("sp", cv[:], cvec, [], ["cv"])
                tr.op("act", ["cv"], ["scb"], lambda: nc.scalar.activation(out=scb[:], in_=cv[:], func=AF.Silu))
                ci = 0
                for l in layers:
                    tr.dma("sp", adab[:], ada_b[l], [], ["adab"])
                    for j in range(6):
                        w = wch[ci % 2]
                        wk = "wch%d" % (ci % 2)
                        ci += 1
                        tr.dma("pool", w[:], ada_w[l, :, :, 1024 * j:1024 * (j + 1)], [], [wk])
                        b = bank()

                        def mm(w=w, b=b):
                            inst = None
                            for ot in range(8):
                                for k in range(KT):
                                    inst = nc.tensor.matmul(ps[b][:, 3 * ot:3 * ot + 3], w[:, k, 128 * ot:128 * (ot + 1)],
                                                            scb[:, k, :], start=(k == 0), stop=(k == KT - 1))
                            return inst
                        tr.op("pe", [wk, "scb"], [("ps", b)], mm)
                        tr.op("dve", [("ps", b), "adab"], [("mods", l)],
                              lambda l=l, j=j, b=b: nc.vector.tensor_tensor(
                                  out=mods[l][:, 8 * j:8 * j + 8, :],
                                  in0=ps[b][:, 0:24].rearrange("p (a c) -> p a c", c=3),
                                  in1=adab[:, 8 * j:8 * j + 8].unsqueeze(2).to_broadcast([128, 8, 3]), op=ALU.add))
                    for (ms, gs, gk, o0) in [(ms1, n1g_s, "n1g", 8), (ms2, n2g_s, "n2g", 32)]:
                        tr.op("dve", [("mods", l)], [("ms", l, o0)],
                              lambda ms=ms, o0=o0, l=l: nc.vector.tensor_scalar(
                                  out=ms[l][:], in0=mods[l][:, o0:o0 + 8, :], scalar1=1.0, scalar2=None, op0=ALU.add))
                        tr.op("dve", [("ms", l, o0), gk], [("ms", l, o0)],
                              lambda ms=ms, gs=gs, l=l: nc.vector.tensor_tensor(
                                  out=ms[l][:], in0=ms[l][:],
                                  in1=gs[:, l, :].unsqueeze(2).to_broadcast([128, KT, 3]), op=ALU.mult))
                tr.barrier()

        phase_mods()

        W1d = nc.dram_tensor("W1d", [DEPTH, 128, 32, 2, 128], BF16).ap()
        W3d = nc.dram_tensor("W3d", [DEPTH, 128, 2, 32, 128], BF16).ap()
        W4d = nc.dram_tensor("W4d", [DEPTH, 128, 32, 128], BF16).ap()
        AP1d = nc.dram_tensor("AP1d", [DEPTH, 128, 17, 2, 32], F32).ap()
        AP2d = nc.dram_tensor("AP2d", [DEPTH, 128, 17, 2, 32], F32).ap()

        def s5_gen(l):
            with contextlib.ExitStack() as sg_:
                W3 = sb("W3", [128, 2, 32, 128], BF16, sg_)
                W4 = sb("W4", [128, 32, 128], BF16, sg_)
                AP1 = sb("AP1", [128, 17, 2, 32], F32, sg_)
                AP2 = sb("AP2", [128, 17, 2, 32], F32, sg_)
                tr.op("dve", [], ["AP1"], lambda: nc.vector.memset(AP1[:, 0, :, :], 1.0))
                tr.op("dve", [], ["AP2"], lambda: nc.vector.memset(AP2[:, 0, :, :], 0.0))
                scope("C_gen", sg_)
                def t32(name, shape=(128, 32)):
                    return sb(name, list(shape), F32, sg_)
                lr = t32("lr"); li_ = t32("li"); ldt = t32("ldt"); dsel = t32("dsel")
                br = t32("br", (128, 32, 16)); bi = t32("bi", (128, 32, 16))
                cr = t32("cr", (128, 32, 16)); ci_ = t32("ci", (128, 32, 16))
                for (dst, src, key) in [(lr, lamre, "lr"), (li_, lamim, "li"), (ldt, logdt, "ldt"), (dsel, s5d, "dsel"),
                                        (br, bre, "br"), (bi, bim, "bi"), (cr, cre, "cr"), (ci_, cim, "ci")]:
                    tr.dma("sp", dst[:], src[l], [], [key])
                dt_ = t32("dt"); xr = t32("xr"); th = t32("th"); mag = t32("mag")
                cs = t32("cs"); sn = t32("sn"); abr = t32("abr"); abi = t32("abi")
                t1 = t32("t1"); t2 = t32("t2"); t3 = t32("t3"); kre = t32("kre"); kim = t32("kim")
                bbr = t32("bbr", (128, 32, 16)); bbi = t32("bbi", (128, 32, 16)); tb = t32("tb", (128, 32, 16))
                ex = t32("ex", (128, 8, 32)); eth = t32("eth", (128, 8, 32))
                pmag = t32("pmag", (128, 8, 32)); nmag = t32("nmag", (128, 8, 32))
                psn = t32("psn", (128, 8, 32)); pcs = t32("pcs", (128, 8, 32))
                pwr = t32("pwr", (128, 8, 32)); pwi = t32("pwi", (128, 8, 32))
                nwr = t32("nwr", (128, 8, 32)); nwi = t32("nwi", (128, 8, 32))
                spw = contextlib.ExitStack()
                rt1 = sb("rt1", [128, 512], F32, spw); rti = sb("rti", [128, 512], I32, spw); rt2 = sb("rt2", [128, 512], F32, spw)
                V = nc.vector

                def dv(reads, writes, fn):
                    tr.op("dve", reads, writes, fn)

                def sin_of(arg, argk, out, outk, shift, nel):
                    a1 = rt1[:, 0:nel]; ai = rti[:, 0:nel]; a2 = rt2[:, 0:nel]
                    dv([argk], ["rt1"], lambda: V.tensor_scalar(out=a1, in0=arg, scalar1=shift, scalar2=1.0 / TWO_PI, op0=ALU.add, op1=ALU.mult))
                    dv(["rt1"], ["rti"], lambda: V.tensor_copy(out=ai, in_=a1))
                    dv(["rti"], ["rt2"], lambda: V.tensor_copy(out=a2, in_=ai))
                    dv(["rt2", argk], ["rt1"], lambda: V.scalar_tensor_tensor(out=a1, in0=a2, scalar=-TWO_PI, in1=arg, op0=ALU.mult, op1=ALU.add))
                    dv(["rt1"], ["rt2"], lambda: V.tensor_scalar(out=a2, in0=a1, scalar1=shift, scalar2=math.pi, op0=ALU.add, op1=ALU.min))
                    dv(["rt2"], ["rt1"], lambda: V.tensor_scalar(out=a1, in0=a2, scalar1=-math.pi, scalar2=None, op0=ALU.max))
                    tr.op("act", ["rt1"], [outk], lambda: nc.scalar.activation(out=out, in_=a1, func=AF.Sin))

                def tt(out, outk, a, ak, b, bk, op):
                    dv([ak, bk], [outk], lambda: V.tensor_tensor(out=out, in0=a, in1=b, op=op))

                tr.op("act", ["ldt"], ["dt"], lambda: nc.scalar.activation(out=dt_[:], in_=ldt[:], func=AF.Exp))
                tt(xr[:], "xr", dt_[:], "dt", lr[:], "lr", ALU.mult)
                tt(th[:], "th", dt_[:], "dt", li_[:], "li", ALU.mult)
                tr.op("act", ["xr"], ["mag"], lambda: nc.scalar.activation(out=mag[:], in_=xr[:], func=AF.Exp))
                sin_of(th[:], "th", sn[:], "sn", 0.0, 32)
                sin_of(th[:], "th", cs[:], "cs", math.pi / 2, 32)
                tt(abr[:], "abr", mag[:], "mag", cs[:], "cs", ALU.mult)
                tt(abi[:], "abi", mag[:], "mag", sn[:], "sn", ALU.mult)
                tt(t1[:], "t1", lr[:], "lr", lr[:], "lr", ALU.mult)
                tt(t2[:], "t2", li_[:], "li", li_[:], "li", ALU.mult)
                tt(t1[:], "t1", t1[:], "t1", t2[:], "t2", ALU.add)
                dv(["t1"], ["t3"], lambda: V.reciprocal(out=t3[:], in_=t1[:]))
                dv(["abr"], ["t1"], lambda: V.tensor_scalar(out=t1[:], in0=abr[:], scalar1=-1.0, scalar2=None, op0=ALU.add))
                tt(t2[:], "t2", t1[:], "t1", lr[:], "lr", ALU.mult)
                tt(kre[:], "kre", abi[:], "abi", li_[:], "li", ALU.mult)
                tt(kre[:], "kre", kre[:], "kre", t2[:], "t2", ALU.add)
                tt(kre[:], "kre", kre[:], "kre", t3[:], "t3", ALU.mult)
                tt(t2[:], "t2", t1[:], "t1", li_[:], "li", ALU.mult)
                tt(kim[:], "kim", abi[:], "abi", lr[:], "lr", ALU.mult)
                tt(kim[:], "kim", kim[:], "kim", t2[:], "t2", ALU.subtract)
                tt(kim[:], "kim", kim[:], "kim", t3[:], "t3", ALU.mult)
                kreb = kre[:].unsqueeze(2).to_broadcast([128, 32, 16])
                kimb = kim[:].unsqueeze(2).to_broadcast([128, 32, 16])
                tt(bbr[:], "bbr", br[:], "br", kreb, "kre", ALU.mult)
                tt(tb[:], "tb", bi[:], "bi", kimb, "kim", ALU.mult)
                tt(bbr[:], "bbr", bbr[:], "bbr", tb[:], "tb", ALU.subtract)
                tt(bbi[:], "bbi", bi[:], "bi", kreb, "kre", ALU.mult)
                tt(tb[:], "tb", br[:], "br", kimb, "kim", ALU.mult)
                tt(bbi[:], "bbi", bbi[:], "bbi", tb[:], "tb", ALU.add)
                tt(ex[:], "ex", evals[:], "evals", xr[:].unsqueeze(1).to_broadcast([128, 8, 32]), "xr", ALU.mult)
                tt(eth[:], "eth", evals[:], "evals", th[:].unsqueeze(1).to_broadcast([128, 8, 32]), "th", ALU.mult)
                tr.op("act", ["ex"], ["pmag"], lambda: nc.scalar.activation(out=pmag[:], in_=ex[:], func=AF.Exp))
                tr.op("act", ["ex"], ["nmag"], lambda: nc.scalar.activation(out=nmag[:], in_=ex[:], func=AF.Exp, scale=-1.0))
                fl = lambda t: t[:].rearrange("p a b -> p (a b)")
                sin_of(fl(eth), "eth", fl(psn), "psn", 0.0, 256)
                sin_of(fl(eth), "eth", fl(pcs), "pcs", math.pi / 2, 256)
                tt(pwr[:], "pwr", pmag[:], "pmag", pcs[:], "pcs", ALU.mult)
                tt(pwi[:], "pwi", pmag[:], "pmag", psn[:], "psn", ALU.mult)
                tt(nwr[:], "nwr", nmag[:], "nmag", pcs[:], "pcs", ALU.mult)
                dv(["nmag", "psn"], ["nwi"], lambda: V.scalar_tensor_tensor(out=nwi[:], in0=nmag[:], scalar=-1.0, in1=psn[:], op0=ALU.mult, op1=ALU.mult))
                t16 = lambda nm: sb(nm, [128, 16, 32], F32, spw)
                jv = t16("jv")
                tr.dma("sp", jv[:], jvals_d, [], ["jv"])
                ex16 = t16("ex16"); eth16 = t16("eth16"); pm16 = t16("pm16")
                sn16 = t16("sn16"); cs16 = t16("cs16")
                tt(ex16[:], "ex16", jv[:], "jv", xr[:].unsqueeze(1).to_broadcast([128, 16, 32]), "xr", ALU.mult)
                tt(eth16[:], "eth16", jv[:], "jv", th[:].unsqueeze(1).to_broadcast([128, 16, 32]), "th", ALU.mult)
                tr.op("act", ["ex16"], ["pm16"], lambda: nc.scalar.activation(out=pm16[:], in_=ex16[:], func=AF.Exp))
                sin_of(fl(eth16), "eth16", fl(sn16), "sn16", 0.0, 512)
                sin_of(fl(eth16), "eth16", fl(cs16), "cs16", math.pi / 2, 512)
                tt(cs16[:], "cs16", cs16[:], "cs16", pm16[:], "pm16", ALU.mult)
                tt(sn16[:], "sn16", sn16[:], "sn16", pm16[:], "pm16", ALU.mult)
                dv(["cs16"], ["AP1"], lambda: V.tensor_copy(out=AP1[:, 1:17, 0, :], in_=cs16[:]))
                dv(["cs16"], ["AP1"], lambda: V.tensor_copy(out=AP1[:, 1:17, 1, :], in_=cs16[:]))
                dv(["sn16"], ["AP2"], lambda: V.tensor_copy(out=AP2[:, 1:17, 1, :], in_=sn16[:]))
                dv(["sn16"], ["AP2"], lambda: V.tensor_scalar(out=AP2[:, 1:17, 0, :], in0=sn16[:], scalar1=-1.0, scalar2=None, op0=ALU.mult))
                tr.barrier()
                spw.close()
                BN = sb("BN", [128, 2, 32, 128], BF16, sg_)
                P1 = t32("P1", (128, 16, 8, 16)); P2 = t32("P2", (128, 16, 8, 16))
                W1 = sb("W1", [128, 32, 2, 128], BF16, sg_)
                w4a = t32("w4a", (128, 128)); w4b = t32("w4b", (128, 128))
                for dd in range(2):
                    tl = slice(16 * dd, 16 * dd + 16)

                    def pw(t):
                        return t[:, :, tl].rearrange("p r t -> p t r").unsqueeze(3).to_broadcast([128, 16, 8, 16])

                    def bc(t):
                        return t[:, tl, :].unsqueeze(2).to_broadcast([128, 16, 8, 16])

                    def outv(t, pl):
                        return t[:, pl, tl, :].rearrange("p t (r h) -> p t r h", h=16)
                    tt(P1[:], "P1", pw(nwr), "nwr", bc(bbr), "bbr", ALU.mult)
                    tt(P2[:], "P2", pw(nwi), "nwi", bc(bbi), "bbi", ALU.mult)
                    tt(outv(BN, 0), ("BN", dd), P1[:], "P1", P2[:], "P2", ALU.subtract)
                    tt(P1[:], "P1", pw(nwr), "nwr", bc(bbi), "bbi", ALU.mult)
                    tt(P2[:], "P2", pw(nwi), "nwi", bc(bbr), "bbr", ALU.mult)
                    tt(outv(BN, 1), ("BN", dd), P1[:], "P1", P2[:], "P2", ALU.add)
                    tt(P1[:], "P1", pw(pwr), "pwr", bc(cr), "cr", ALU.mult)
                    tt(P2[:], "P2", pw(pwi), "pwi", bc(ci_), "ci", ALU.mult)
                    tt(outv(W3, 0), ("W3", dd), P1[:], "P1", P2[:], "P2", ALU.subtract)
                    tt(P1[:], "P1", pw(pwr), "pwr", bc(ci_), "ci", ALU.mult)
                    tt(P2[:], "P2", pw(pwi), "pwi", bc(cr), "cr", ALU.mult)
                    dv(["P1", "P2"], [("W3", dd)], lambda: V.scalar_tensor_tensor(out=outv(W3, 1), in0=P1[:], scalar=-1.0, in1=P2[:], op0=ALU.mult, op1=ALU.subtract))
                for grp in range(8):
                    def trp(grp=grp):
                        inst = None
                        for j in range(8):
                            idx = grp * 8 + j
                            tau, pl = idx // 2, idx % 2
                            inst = nc.tensor.transpose(psb[:, 128 * j:128 * (j + 1)], BN[:, pl, tau, :], identb[:])
                        return inst
                    tr.op("pe", [("BN", 0), ("BN", 1), "identb"], ["psb"], trp)
                    tr.op("act", ["psb"], [("W1", grp)],
                          lambda grp=grp: nc.scalar.activation(out=W1[:, 4 * grp:4 * grp + 4, :, :].rearrange("p t c m -> p (t c m)"), in_=psb[:, :], func=AF.Copy))
                for g in range(32):
                    gp, pair = g % 2, g // 2
                    R = slice(64 * gp, 64 * gp + 64)
                    bf_, br_ = bank(), bank()

                    def mm(bf_=bf_, br_=br_, R=R, pair=pair):
                        inst = None
                        for (bb, tau) in [(bf_, pair), (br_, 16 + pair)]:
                            nc.tensor.matmul(ps[bb][:, 0:128], BN[R, 0, tau, :], W3[R, 0, tau, :], start=True, stop=False)
                            inst = nc.tensor.matmul(ps[bb][:, 0:128], BN[R, 1, tau, :], W3[R, 1, tau, :], start=False, stop=True)
                        return inst
                    tr.op("pe", [("BN", 0), ("BN", 1), ("W3", 0), ("W3", 1)], [("ps", bf_), ("ps", br_)], mm)
                    dv([("ps", bf_), "negmask"], ["w4a"], lambda bf_=bf_: V.tensor_tensor(out=w4a[:], in0=ps[bf_][:, 0:128], in1=negmask[:, 0, :], op=ALU.mult))
                    dv([("ps", br_), "negmask"], ["w4b"], lambda br_=br_: V.tensor_tensor(out=w4b[:], in0=ps[br_][:, 0:128], in1=negmask[:, 1, :], op=ALU.mult))
                    tt(w4a[:], "w4a", w4a[:], "w4a", w4b[:], "w4b", ALU.add)
                    dv(["w4a", "dsel", "identf"], [("W4", g)], lambda g=g: V.scalar_tensor_tensor(out=W4[:, g, :], in0=identf[:], scalar=dsel[:, g:g + 1], in1=w4a[:], op0=ALU.mult, op1=ALU.add))
                tr.barrier()
                tr.dma("sp", W1d[l], W1[:], [], [])
                tr.dma("sp", W3d[l], W3[:], [], [])
                tr.dma("sp", W4d[l], W4[:], [], [])
                tr.dma("sp", AP1d[l], AP1[:], [], [])
                tr.dma("sp", AP2d[l], AP2[:], [], [])
                tr.barrier()

        for l_ in layers:
            s5_gen(l_)


        ncount = [0]

        def norm_tile(st_bufs, hsrc, t0, n, scale_ap, shift_ap, out_fn, out_key, hkey_reads):
            hTs, sqs, sd, rstd, tmp = st_bufs
            i = ncount[0] % 2
            ncount[0] += 1
            hT, sq = hTs[i], sqs[i]
            hk, sk = "hTn%d" % i, "sqn%d" % i
            tr.dma("sp", hT[:, :, 0:n], hsrc[:, :, t0:t0 + n], hkey_reads, [hk])
            tr.op("act", [hk], [sk], lambda: nc.scalar.activation(out=sq[:, :, 0:n], in_=hT[:, :, 0:n], func=AF.Square))
            b = bank()

            def mm():
                inst = None
                for k in range(KT):
                    inst = nc.tensor.matmul(ps[b][:, 0:n], onesb[:], sq[:, k, 0:n], start=(k == 0), stop=(k == KT - 1))
                return inst
            tr.op("pe", [sk, "onesb"], [("ps", b)], mm)
            tr.op("act", [("ps", b), "epsc"], ["sd"],
                  lambda: nc.scalar.activation(out=sd[:, 0:n], in_=ps[b][:, 0:n], func=AF.Sqrt, bias=epsc[:, 0:1], scale=1.0 / D))
            tr.op("dve", ["sd"], ["rstd"], lambda: nc.vector.reciprocal(out=rstd[:, 0:n], in_=sd[:, 0:n]))
            tr.op("dve", [hk, "rstd"], ["tmpn"],
                  lambda: nc.vector.tensor_tensor(out=tmp[:, :, 0:n], in0=hT[:, :, 0:n],
                                                  in1=rstd[:, 0:n].unsqueeze(1).to_broadcast([128, KT, n]), op=ALU.mult))

            def ev():
                inst = None
                for k in range(KT):
                    inst = nc.scalar.activation(out=out_fn(k), in_=tmp[:, k, 0:n], func=AF.Identity,
                                                scale=scale_ap(k), bias=shift_ap(k))
                return inst
            tr.op("act", ["tmpn"], [out_key], ev)

        def norm_sb(bufs, h_, hk, n, scale_ap, shift_ap, out_fn, out_key):
            sq, sd, rstd = bufs
            tr.op("act", [hk], ["sqS"], lambda: nc.scalar.activation(out=sq[:, :, 0:n], in_=h_[:, :, 0:n], func=AF.Square))
            b = bank()

            def mm():
                inst = None
                for k in range(KT):
                    inst = nc.tensor.matmul(ps[b][:, 0:n], onesb[:], sq[:, k, 0:n], start=(k == 0), stop=(k == KT - 1))
                return inst
            tr.op("pe", ["sqS", "onesb"], [("ps", b)], mm)
            tr.op("act", [("ps", b), "epsc"], ["sdS"],
                  lambda: nc.scalar.activation(out=sd[:, 0:n], in_=ps[b][:, 0:n], func=AF.Sqrt, bias=epsc[:, 0:1], scale=1.0 / D))
            tr.op("dve", ["sdS"], ["rstdS"], lambda: nc.vector.reciprocal(out=rstd[:, 0:n], in_=sd[:, 0:n]))
            tr.op("dve", [hk, "rstdS"], [hk],
                  lambda: nc.vector.tensor_tensor(out=h_[:, :, 0:n], in0=h_[:, :, 0:n],
                                                  in1=rstd[:, 0:n].unsqueeze(1).to_broadcast([128, KT, n]), op=ALU.mult))

            def ev():
                inst = None
                for k in range(KT):
                    inst = nc.scalar.activation(out=out_fn(k), in_=h_[:, k, 0:n], func=AF.Identity,
                                                scale=scale_ap(k), bias=shift_ap(k))
                return inst
            tr.op("act", [hk], [out_key], ev)

        def alloc_norm_sb(st):
            return (sb("sqS", [128, KT, 512], BF16, st), sb("sdS", [128, 512], F32, st), sb("rstdS", [128, 512], F32, st))

        def alloc_norm_bufs(st):
            return ([sb("hTn%d" % i, [128, KT, 512], F32, st) for i in range(2)], [sb("sqn%d" % i, [128, KT, 512], BF16, st) for i in range(2)],
                    sb("sdn", [128, 512], F32, st), sb("rstdn", [128, 512], F32, st), sb("tmpn", [128, KT, 512], F32, st))

        for s in range(nseq):
            for li, l in enumerate(layers):
                hsrc = xT[s] if li == 0 else hbuf[s]
                hdst = hbuf[s]
                jl = s

                def modcol(isctx):
                    return 2 if isctx else jl

                with contextlib.ExitStack() as mix:
                    uT = sb("uT", [128, 4, LT], BF16, mix)
                    ybT = sb("ybT", [128, 4, LT], BF16, mix)
                    with contextlib.ExitStack() as st:
                        aT = sb("aT", [128, 4, APAD], BF16, st)
                        with contextlib.ExitStack() as sa:
                            scope("A", sa)
                            nb = alloc_norm_bufs(sa) if li == 0 else None
                            wA = sb("wA", [128, 12, KT, 128], BF16, sa)
                            nlt = [sb("nlt%d" % i, [128, KT, 512], BF16, sa) for i in range(3 if li > 0 else 2)]
                            sg = sb("sgA", [128, 4, 512], BF16, sa)
                            for o_ in [0, 1, 2, 3, 8, 4, 9, 5, 10, 6, 11, 7]:
                                tr.dma("pool", wA[:, o_, :, :], w_inA[l, o_], [], [("wA", o_)])
                            for (a, b_) in [(0, 16), (272, 288), (2336, 2352)]:
                                tr.op("pool", [], ["aT"], lambda a=a, b_=b_: nc.gpsimd.memset(aT[:, :, a:b_], 0.0))
                            for ti, (t0, n, isc) in enumerate(TILES):
                                mc = modcol(isc)
                                nl = nlt[ti % len(nlt)]
                                nk = "nlt%d" % (ti % len(nlt))
                                if li == 0:
                                    norm_tile(nb, hsrc, t0, n, lambda k, mc=mc: ms1[l][:, k, mc:mc + 1],
                                              lambda k, mc=mc: mods[l][:, k, mc:mc + 1], lambda k, nl=nl, n=n: nl[:, k, 0:n], nk, [])
                                    tr.dma("sp", nlbuf[:, :, t0:t0 + n], nl[:, :, 0:n], [nk], [("nlbuf", ti)])
                                else:
                                    tr.dma("sp", nl[:, :, 0:n], nlbuf[:, :, t0:t0 + n], [], [nk])
                                p0 = apos(t0)
                                for o in [0, 1, 2, 3, 8, 4, 9, 5, 10, 6, 11, 7]:
                                    b = bank()

                                    def mm(o=o, b=b, nl=nl, n=n):
                                        inst = None
                                        for k in range(KT):
                                            inst = nc.tensor.matmul(ps[b][:, 0:n], wA[:, o, k, :], nl[:, k, 0:n],
                                                                    start=(k == 0), stop=(k == KT - 1))
                                        return inst
                                    tr.op("pe", [("wA", o), nk], [("ps", b)], mm)
                                    if o < 4:
                                        tr.op("act", [("ps", b)], [("uT", ti)],
                                              lambda o=o, b=b, t0=t0, n=n: nc.scalar.activation(
                                                  out=uT[:, o, :].rearrange("p (r q) -> p r q", r=8)[:, :, t0 // 8:(t0 + n) // 8],
                                                  in_=ps[b][:, 0:n].rearrange("p (q r) -> p r q", r=8), func=AF.Copy))
                                    elif o >= 8:
                                        tr.op("act", [("ps", b)], [("sgA", o - 8)],
                                              lambda o=o, b=b, n=n: nc.scalar.activation(out=sg[:, o - 8, 0:n], in_=ps[b][:, 0:n], func=AF.Sigmoid))
                                    else:
                                        tr.op("dve", [("ps", b), ("sgA", o - 4)], ["aT"],
                                              lambda o=o, b=b, n=n, p0=p0: nc.vector.tensor_tensor(
                                                  out=aT[:, o - 4, p0:p0 + n], in0=ps[b][:, 0:n], in1=sg[:, o - 4, 0:n], op=ALU.mult))
                        tr.barrier()
                        if debug and s == 0 and li == 0:
                            tr.dma("sp", dbg_out("uT", [128, 4, LT], BF16), uT[:], [], [])
                            tr.dma("sp", dbg_out("aT", [128, 4, APAD], BF16), aT[:], [], [])
                            tr.barrier()
                        with contextlib.ExitStack() as sbk:
                            scope("B", sbk)
                            dg = sb("dg31", [128, 4, 31, 128], BF16, sbk)
                            x32 = sb("x32", [128, 4, 512], F32, sbk)
                            xb = sb("xb", [128, 4, 512], BF16, sbk)
                            xsq = sb("xsq", [128, 4, 512], BF16, sbk)
                            mu = sb("mu", [128, 512], F32, sbk)
                            var = sb("var", [128, 512], F32, sbk)
                            rs = sb("rs", [128, 512], F32, sbk)
                            for c in range(4):
                                def mk(c=c):
                                    inst = None
                                    for k in range(31):
                                        inst = nc.vector.tensor_scalar(out=dg[:, c, k, :], in0=identf[:], scalar1=cdw_s[:, l, c, k:k + 1],
                                                                       scalar2=None, op0=ALU.mult)
                                    return inst
                                tr.op("dve", [], [("dg", c)], mk)
                            for ti, (t0, n, isc) in enumerate(TILES):
                                p0 = apos(t0)
                                for c in range(4):
                                    b = bank()

                                    def mm(c=c, b=b, n=n, p0=p0):
                                        inst = None
                                        for k in range(31):
                                            inst = nc.tensor.matmul(ps[b][:, 0:n], dg[:, c, k, :], aT[:, c, p0 + k - 15:p0 + k - 15 + n],
                                                                    start=(k == 0), stop=(k == 30))
                                        return inst
                                    tr.op("pe", [("dg", c), "aT"], [("ps", b)], mm)
                                    tr.op("act", [("ps", b)], [("x32", c)],
                                          lambda c=c, b=b, n=n: nc.scalar.activation(out=x32[:, c, 0:n], in_=ps[b][:, 0:n], func=AF.Identity,
                                                                                     bias=cdb_s[:, l, c:c + 1], scale=1.0))
                                    tr.op("act", [("ps", b)], [("xsq", c)],
                                          lambda c=c, b=b, n=n: nc.scalar.activation(out=xsq[:, c, 0:n], in_=ps[b][:, 0:n], func=AF.Square,
                                                                                     bias=cdb_s[:, l, c:c + 1], scale=1.0))
                                    tr.op("dve", [("x32", c)], [("xb", c)],
                                          lambda c=c, n=n: nc.vector.tensor_copy(out=xb[:, c, 0:n], in_=x32[:, c, 0:n]))
                                b1 = bank()
                                b2 = bank()

                                def mm2(b1=b1, b2=b2, n=n):
                                    inst = None
                                    for c in range(4):
                                        nc.tensor.matmul(ps[b1][:, 0:n], o512b[:], xb[:, c, 0:n], start=(c == 0), stop=(c == 3))
                                    for c in range(4):
                                        inst = nc.tensor.matmul(ps[b2][:, 0:n], o512b[:], xsq[:, c, 0:n], start=(c == 0), stop=(c == 3))
                                    return inst
                                tr.op("pe", [("xb", c) for c in range(4)] + [("xsq", c) for c in range(4)] + ["o512b"], [("ps", b1), ("ps", b2)], mm2)
                                tr.op("act", [("ps", b1)], ["mu"], lambda b1=b1, n=n: nc.scalar.activation(out=mu[:, 0:n], in_=ps[b1][:, 0:n], func=AF.Copy))
                                tr.op("dve", ["mu"], ["var"], lambda n=n: nc.vector.tensor_tensor(out=var[:, 0:n], in0=mu[:, 0:n], in1=mu[:, 0:n], op=ALU.mult))
                                tr.op("dve", ["var", ("ps", b2)], ["var"],
                                      lambda b2=b2, n=n: nc.vector.tensor_tensor(out=var[:, 0:n], in0=ps[b2][:, 0:n], in1=var[:, 0:n], op=ALU.subtract))
                                tr.op("dve", ["var"], ["var"], lambda n=n: nc.vector.tensor_scalar(out=var[:, 0:n], in0=var[:, 0:n], scalar1=0.0, scalar2=None, op0=ALU.max))
                                tr.op("act", ["var", "epsc"], ["rs"],
                                      lambda n=n: nc.scalar.activation(out=rs[:, 0:n], in_=var[:, 0:n], func=AF.Sqrt, bias=epsc[:, 0:1], scale=1.0))
                                tr.op("dve", ["rs"], ["rs"], lambda n=n: nc.vector.reciprocal(out=rs[:, 0:n], in_=rs[:, 0:n]))
                                allx = [("x32", c) for c in range(4)]
                                tr.op("dve", allx + ["mu"], allx,
                                      lambda n=n: nc.vector.tensor_tensor(out=x32[:, :, 0:n], in0=x32[:, :, 0:n],
                                                                          in1=mu[:, 0:n].unsqueeze(1).to_broadcast([128, 4, n]), op=ALU.subtract))
                                tr.op("dve", allx + ["rs"], allx,
                                      lambda n=n: nc.vector.tensor_tensor(out=x32[:, :, 0:n], in0=x32[:, :, 0:n],
                                                                          in1=rs[:, 0:n].unsqueeze(1).to_broadcast([128, 4, n]), op=ALU.mult))

                                def ev(t0=t0, n=n):
                                    inst = None
                                    for c in range(4):
                                        inst = nc.scalar.activation(out=ybT[:, c, t0:t0 + n], in_=x32[:, c, 0:n], func=AF.Silu,
                                                                    scale=clg_s[:, l, c:c + 1], bias=clb_s[:, l, c:c + 1])
                                    return inst
                                tr.op("act", allx, [("ybT", ti)], ev)
                        tr.barrier()
                    if debug and s == 0 and li == 0:
                        tr.dma("sp", dbg_out("ybT", [128, 4, LT], BF16), ybT[:], [], [])
                        tr.barrier()

                    with contextlib.ExitStack() as sc:
                        Z = sb("Z", [128, 2, 32, NQ], BF16, sc)
                        Ucol = sb("Ucol", [128, 32, NQ], BF16, sc)
                        W3 = sb("W3", [128, 2, 32, 128], BF16, sc)
                        W4 = sb("W4", [128, 32, 128], BF16, sc)
                        AP1 = sb("AP1", [128, 17, 2, 32], F32, sc)
                        AP2 = sb("AP2", [128, 17, 2, 32], F32, sc)
                        with contextlib.ExitStack() as ssel:
                            scope("C_im2col", ssel)
                            sel = sb("sel", [128, 64, 128], BF16, ssel)
                            for gq_ in range(8):
                                tr.dma("pool", sel[:, 8 * gq_:8 * gq_ + 8, :], sel1_d[:, 8 * gq_:8 * gq_ + 8, :], [], [("sel", gq_)])
                            V = nc.vector
                            for g in range(32):
                                c, gq = g // 8, g % 8
                                b = bank()

                                def mm(c=c, gq=gq, b=b):
                                    inst = None
                                    ur = uT[:, c, :].rearrange("p (r q) -> p r q", r=8)
                                    for r in range(8):
                                        inst = nc.tensor.matmul(ps[b][:, 0:NQ], sel[:, 8 * gq + r, :], ur[:, r, :], start=(r == 0), stop=(r == 7))
                                    return inst
                                tr.op("pe", [("sel", gq)] + [("uT", ti) for ti in range(5)], [("ps", b)], mm)
                                if g % 2 == 0:
                                    tr.op("act", [("ps", b)], [("Ucol", g)], lambda g=g, b=b: nc.scalar.activation(out=Ucol[:, g, :], in_=ps[b][:, 0:NQ], func=AF.Copy))
                                else:
                                    tr.op("dve", [("ps", b)], [("Ucol", g)], lambda g=g, b=b: V.tensor_copy(out=Ucol[:, g, :], in_=ps[b][:, 0:NQ]))
                        tr.barrier()
                        with contextlib.ExitStack() as sz:
                            W1 = sb("W1", [128, 32, 2, 128], BF16, sz)
                            V = nc.vector
                            tr.dma("sp", W1[:], W1d[l], [], [("W1", g_) for g_ in range(8)])
                            tr.dma("sp", W3[:], W3d[l], [], ["W3"])
                            tr.dma("sp", W4[:], W4d[l], [], ["W4"])
                            tr.dma("sp", AP1[:], AP1d[l], [], ["AP1"])
                            tr.dma("sp", AP2[:], AP2d[l], [], ["AP2"])
                            for tau in range(32):
                                dd, pair = tau // 16, tau % 16
                                bz = [bank(), bank()]

                                def mm(tau=tau, pair=pair, bz=bz):
                                    inst = None
                                    for pl in range(2):
                                        for gp in range(2):
                                            inst = nc.tensor.matmul(ps[bz[pl]][64 * gp:64 * gp + 64, 0:NQ], W1[:, tau, pl, 64 * gp:64 * gp + 64],
                                                                    Ucol[:, 2 * pair + gp, :], start=True, stop=True)
                                    return inst
                                tr.op("pe", [("W1", tau // 4), ("Ucol", 2 * pair), ("Ucol", 2 * pair + 1)], [("ps", bz[0]), ("ps", bz[1])], mm)
                                for pl in range(2):
                                    en = "act" if pl == 0 else "dve"

                                    def cp(o, i, en=en):
                                        if en == "act":
                                            return nc.scalar.activation(out=o, in_=i, func=AF.Copy)
                                        return V.tensor_copy(out=o, in_=i)
                                    if dd == 0:
                                        tr.op(en, [("ps", bz[pl])], [("Z", tau)], lambda pl=pl, tau=tau, bz=bz, cp=cp: cp(Z[:, pl, tau, :], ps[bz[pl]][:, 0:NQ]))
                                    else:
                                        def ev(pl=pl, tau=tau, bz=bz, cp=cp):
                                            cp(Z[:, pl, tau, 0:QC], ps[bz[pl]][:, QC - 1::-1])
                                            return cp(Z[:, pl, tau, QC:NQ], ps[bz[pl]][:, NQ - 1:QC - 1:-1])
                                        tr.op(en, [("ps", bz[pl])], [("Z", tau)], ev)
                        tr.barrier()
                        with contextlib.ExitStack() as ss:
                            scope("C_scan", ss)
                            TS = 24
                            for (eng, en, ta, tb_) in [(nc.vector, "dve", 0, TS), (nc.gpsimd, "pool", TS, 32)]:
                                nt = tb_ - ta
                                shp = [128, 2, nt, 18]
                                cur = sb("cur" + en, shp, F32, ss)
                                T32 = [sb("T32%s%d" % (en, i), shp, F32, ss) for i in range(2)]
                                M1 = sb("M1" + en, shp, F32, ss)
                                M2 = sb("M2" + en, shp, F32, ss)
                                Eloc = sb("Eloc" + en, shp, F32, ss)
                                Xb = sb("Xb" + en, [128, 2, nt, 19], F32, ss)
                                m1s = sb("m1s" + en, [128, 2, nt], F32, ss)
                                m2s = sb("m2s" + en, [128, 2, nt], F32, ss)
                                K_ = lambda nm: nm + en

                                def Zv(j, ta=ta, tb_=tb_):
                                    return Z[:, :, ta:tb_, :].rearrange("p c t (q j) -> p c t q j", j=16)[:, :, :, :, j]

                                def Ab(tab, j, ta=ta, tb_=tb_, nt=nt):
                                    return tab[:, j, :, ta:tb_].unsqueeze(3).to_broadcast([128, 2, nt, 18])

                                def As(tab, j, ta=ta, tb_=tb_):
                                    return tab[:, j, :, ta:tb_]
                                for j in range(16):
                                    T = T32[j % 2]
                                    Tk = K_("T32%d" % (j % 2))
                                    if j == 0:
                                        tr.op(en, [], [Tk], lambda T=T, eng=eng, Zv=Zv: eng.tensor_copy(out=T[:], in_=Zv(0)))
                                    else:
                                        tr.op(en, [K_("cur")], [Tk], lambda T=T, eng=eng, Zv=Zv, j=j, cur=cur: eng.tensor_tensor(out=T[:], in0=cur[:], in1=Zv(j), op=ALU.add))
                                        tr.op("act", [Tk], [K_("Zv%d" % j)], lambda T=T, Zv=Zv, j=j: nc.scalar.activation(out=Zv(j), in_=T[:], func=AF.Copy))
                                    dst = Eloc if j == 15 else cur
                                    dk = K_("Eloc") if j == 15 else K_("cur")
                                    tr.op(en, [Tk], [K_("M1")], lambda T=T, eng=eng, Ab=Ab, M1=M1: eng.tensor_tensor(out=M1[:], in0=T[:], in1=Ab(AP1, 1), op=ALU.mult))
                                    tr.op(en, [Tk], [K_("M2")], lambda T=T, eng=eng, Ab=Ab, M2=M2: eng.tensor_tensor(out=M2[:], in0=T[:, ::-1, :, :], in1=Ab(AP2, 1), op=ALU.mult))
                                    tr.op(en, [K_("M1"), K_("M2")], [dk], lambda eng=eng, M1=M1, M2=M2, dst=dst: eng.tensor_tensor(out=dst[:], in0=M1[:], in1=M2[:], op=ALU.add))
                                tr.op(en, [], [K_("Xb")], lambda eng=eng, Xb=Xb: eng.memset(Xb[:, :, :, 0], 0.0))
                                for q in range(18):
                                    tr.op(en, [K_("Xb")], [K_("m1s")], lambda eng=eng, Xb=Xb, m1s=m1s, As=As, q=q: eng.tensor_tensor(out=m1s[:], in0=Xb[:, :, :, q], in1=As(AP1, 16), op=ALU.mult))
                                    tr.op(en, [K_("Xb")], [K_("m2s")], lambda eng=eng, Xb=Xb, m2s=m2s, As=As, q=q: eng.tensor_tensor(out=m2s[:], in0=Xb[:, ::-1, :, q], in1=As(AP2, 16), op=ALU.mult))
                                    tr.op(en, [K_("m1s"), K_("m2s")], [K_("m1s")], lambda eng=eng, m1s=m1s, m2s=m2s: eng.tensor_tensor(out=m1s[:], in0=m1s[:], in1=m2s[:], op=ALU.add))
                                    tr.op(en, [K_("m1s"), K_("Eloc")], [K_("Xb")], lambda eng=eng, Xb=Xb, m1s=m1s, Eloc=Eloc, q=q: eng.tensor_tensor(out=Xb[:, :, :, q + 1], in0=m1s[:], in1=Eloc[:, :, :, q], op=ALU.add))
                                for j in range(16):
                                    zk = K_("Zv%d" % j)
                                    if j == 0:
                                        tr.op(en, [K_("Xb"), zk], [zk], lambda eng=eng, Xb=Xb, Zv=Zv: eng.tensor_tensor(out=Zv(0), in0=Zv(0), in1=Xb[:, :, :, 0:18], op=ALU.add))
                                        continue
                                    tr.op(en, [K_("Xb")], [K_("M1")], lambda eng=eng, Xb=Xb, M1=M1, Ab=Ab, j=j: eng.tensor_tensor(out=M1[:], in0=Xb[:, :, :, 0:18], in1=Ab(AP1, j), op=ALU.mult))
                                    tr.op(en, [K_("Xb")], [K_("M2")], lambda eng=eng, Xb=Xb, M2=M2, Ab=Ab, j=j: eng.tensor_tensor(out=M2[:], in0=Xb[:, ::-1, :, 0:18], in1=Ab(AP2, j), op=ALU.mult))
                                    tr.op(en, [K_("M1"), K_("M2")], [K_("M1")], lambda eng=eng, M1=M1, M2=M2: eng.tensor_tensor(out=M1[:], in0=M1[:], in1=M2[:], op=ALU.add))
                                    tr.op(en, [K_("M1"), zk], [zk], lambda eng=eng, M1=M1, Zv=Zv, j=j: eng.tensor_tensor(out=Zv(j), in0=Zv(j), in1=M1[:], op=ALU.add))
                        tr.barrier()
                        for g in range(32):
                            gp, pair = g % 2, g // 2
                            R = slice(64 * gp, 64 * gp + 64)
                            b = bank()

                            def mm(g=g, R=R, pair=pair, b=b):
                                o = ps[b]
                                nc.tensor.matmul(o[:, 0:NQ], W4[:, g, :], Ucol[:, g, :], start=True, stop=False)
                                for pl in range(2):
                                    nc.tensor.matmul(o[:, 0:NQ], W3[R, pl, pair, :], Z[R, pl, pair, :], start=False, stop=False, skip_group_check=True)
                                inst = None
                                for pl in range(2):
                                    tau = 16 + pair
                                    nc.tensor.matmul(o[:, 0:QC], W3[R, pl, tau, :], Z[R, pl, tau, QC - 1::-1], start=False, stop=False, skip_group_check=True)
                                    inst = nc.tensor.matmul(o[:, QC:NQ], W3[R, pl, tau, :], Z[R, pl, tau, NQ - 1:QC - 1:-1], start=False, stop=(pl == 1), skip_group_check=True)
                                return inst
                            tr.op("pe", [], [("ps", b)], mm)
                            if g % 2 == 0:
                                tr.op("act", [("ps", b)], [("Ycol", g)], lambda g=g, b=b: nc.scalar.activation(
                                    out=uT[:, g // 8, (g % 8) * NQ:(g % 8 + 1) * NQ], in_=ps[b][:, 0:NQ], func=AF.Copy))
                            else:
                                tr.op("dve", [("ps", b)], [("Ycol", g)], lambda g=g, b=b: nc.vector.tensor_copy(
                                    out=uT[:, g // 8, (g % 8) * NQ:(g % 8 + 1) * NQ], in_=ps[b][:, 0:NQ]))
                        tr.barrier()
                        if debug and s == 0 and li == 0:
                            tr.dma("sp", dbg_out("Ycol", [128, 4, LT], BF16), uT[:], [], [])
                            tr.barrier()
                        gT = Ucol[:].rearrange("p g q -> p (g q)").rearrange("p (c t) -> p c t", c=4)
                        with contextlib.ExitStack() as sg2:
                            scope("C_un", sg2)
                            sel = sb("sel", [128, 64, 128], BF16, sg2)
                            for r_ in range(8):
                                tr.dma("pool", sel[:, 8 * r_:8 * r_ + 8, :], sel2_d[:, 8 * r_:8 * r_ + 8, :], [], [("sel2", r_)])
                            xg = [sb("xg%d" % i, [128, NQ], F32, sg2) for i in range(4)]
                            tg = [sb("tg%d" % i, [128, NQ], F32, sg2) for i in range(4)]
                            pi_ = 0
                            for c in range(4):
                                for r in range(8):
                                    b = bank()
                                    x_, t_ = xg[pi_ % 4], tg[pi_ % 4]
                                    xk, tk = "xg%d" % (pi_ % 4), "tg%d" % (pi_ % 4)
                                    pi_ += 1

                                    def mm(c=c, r=r, b=b):
                                        inst = None
                                        for gq in range(8):
                                            inst = nc.tensor.matmul(ps[b][:, 0:NQ], sel[:, 8 * r + gq, :], uT[:, c, gq * NQ:(gq + 1) * NQ],
                                                                    start=(gq == 0), stop=(gq == 7))
                                        return inst
                                    tr.op("pe", [("sel2", r)], [("ps", b)], mm)
                                    tr.op("act", [("ps", b)], [xk], lambda x_=x_, b=b: nc.scalar.activation(out=x_[:], in_=ps[b][:, 0:NQ], func=AF.Copy))
                                    tr.op("pool", [xk], [tk], lambda x_=x_, t_=t_: nc.gpsimd.tensor_tensor(out=t_[:], in0=x_[:], in1=x_[:], op=ALU.mult))
                                    tr.op("pool", [tk], [tk], lambda t_=t_: nc.gpsimd.tensor_scalar(out=t_[:], in0=t_[:], scalar1=0.044715, scalar2=1.0, op0=ALU.mult, op1=ALU.add))
                                    tr.op("dve", [tk, xk], [tk], lambda x_=x_, t_=t_: nc.vector.tensor_tensor(out=t_[:], in0=t_[:], in1=x_[:], op=ALU.mult))
                                    tr.op("act", [tk], [tk], lambda t_=t_: nc.scalar.activation(out=t_[:], in_=t_[:], func=AF.Sigmoid, scale=2.0 * math.sqrt(2.0 / math.pi)))
                                    tr.op("dve", [tk, xk], ["gT"], lambda x_=x_, t_=t_, c=c, r=r: nc.vector.tensor_tensor(
                                        out=gT[:, c, :].rearrange("p (q r) -> p r q", r=8)[:, r, :], in0=x_[:], in1=t_[:], op=ALU.mult))
                        tr.barrier()
                        with contextlib.ExitStack() as sg3:
                            scope("C_glu", sg3)
                            wg = sb("wglu", [128, 4, 512], BF16, sg3)
                            sgl = [sb("sgl%d" % i, [128, 512], BF16, sg3) for i in range(2)]
                            tr.dma("pool", wg[:], w_glu[l], [], ["wglu"])
                            pi_ = 0
                            for ti, (t0, n, isc) in enumerate(TILES):
                                for o in range(4):
                                    b = bank()
                                    sl, sk = sgl[pi_ % 2], "sgl%d" % (pi_ % 2)
                                    pi_ += 1

                                    def mm(o=o, b=b, t0=t0, n=n):
                                        inst = None
                                        for k in range(4):
                                            inst = nc.tensor.matmul(ps[b][:, 0:n], wg[:, k, 128 * o:128 * (o + 1)], gT[:, k, t0:t0 + n], start=(k == 0), stop=(k == 3))
                                        return inst
                                    tr.op("pe", ["wglu"], [("ps", b)], mm)
                                    tr.op("act", [("ps", b)], [sk], lambda sl=sl, b=b, n=n: nc.scalar.activation(out=sl[:, 0:n], in_=ps[b][:, 0:n], func=AF.Sigmoid))
                                    tr.op("dve", [sk], [("yaT", ti)], lambda sl=sl, o=o, t0=t0, n=n: nc.vector.tensor_tensor(
                                        out=uT[:, o, t0:t0 + n], in0=gT[:, o, t0:t0 + n], in1=sl[:, 0:n], op=ALU.mult))
                        tr.barrier()
                    if debug and s == 0 and li == 0:
                        tr.dma("sp", dbg_out("yaT", [128, 4, LT], BF16), uT[:], [], [])
                        tr.barrier()

                    with contextlib.ExitStack() as sd_:
                        scope("D", sd_)
                        wa = sb("wa", [128, 8, 4, 128], BF16, sd_)
                        wb_ = sb("wb", [128, 8, 4, 128], BF16, sd_)
                        wgt = sb("wgt", [128, 16, KT, 128], BF16, sd_)
                        wo = sb("wo", [128, 8, KT, 128], BF16, sd_)
                        hT = [sb("hTd%d" % i, [128, KT, 512], F32, sd_) for i in range(2)]
                        nlt = [sb("nld%d" % i, [128, KT, 512], BF16, sd_) for i in range(2)]
                        mT = sb("mT", [128, KT, 512], BF16, sd_)
                        s3 = [sb("s3_%d" % i, [128, 512], F32, sd_) for i in range(2)]
                        s4 = [sb("s4_%d" % i, [128, 512], F32, sd_) for i in range(2)]
                        nsb = alloc_norm_sb(sd_)
                        nlo = sb("nlo", [128, KT, 512], BF16, sd_)
                        for o_ in range(8):
                            tr.dma("pool", wa[:, o_, :, :], w_a[l, o_], [], [("wa", o_)])
                            tr.dma("pool", wb_[:, o_, :, :], w_b[l, o_], [], [("wb", o_)])
                            tr.dma("pool", wgt[:, o_, :, :], w_inG[l, o_], [], [("wgt", o_)])
                            tr.dma("pool", wgt[:, 8 + o_, :, :], w_inG[l, 8 + o_], [], [("wgt", 8 + o_)])
                        for o_ in range(8):
                            tr.dma("pool", wo[:, o_, :, :], w_out[l, o_], [], [("wo", o_)])
                        pi_ = 0
                        for ti, (t0, n, isc) in enumerate(TILES):
                            mc = modcol(isc)
                            h_, hk = hT[ti % 2], "hTd%d" % (ti % 2)
                            nl, nk = nlt[ti % 2], "nld%d" % (ti % 2)

                            def loadD(tj):
                                (ta_, na_, _) = TILES[tj]
                                tr.dma("sp", hT[tj % 2][:, :, 0:na_], hsrc[:, :, ta_:ta_ + na_], [], ["hTd%d" % (tj % 2)])
                                tr.dma("sp", nlt[tj % 2][:, :, 0:na_], nlbuf[:, :, ta_:ta_ + na_], [], ["nld%d" % (tj % 2)])
                            if ti == 0:
                                loadD(0)
                            if ti + 1 < len(TILES):
                                loadD(ti + 1)
                            for o in range(KT):
                                b1, b2, b3, b4 = bank(), bank(), bank(), bank()
                                a3, a4 = s3[pi_ % 2], s4[pi_ % 2]
                                k3, k4 = "s3_%d" % (pi_ % 2), "s4_%d" % (pi_ % 2)
                                pi_ += 1

                                def mm(o=o, b1=b1, b2=b2, b3=b3, b4=b4, t0=t0, n=n, nl=nl):
                                    inst = None
                                    for k in range(4):
                                        nc.tensor.matmul(ps[b1][:, 0:n], wa[:, o, k, :], uT[:, k, t0:t0 + n], start=(k == 0), stop=(k == 3))
                                    for k in range(4):
                                        nc.tensor.matmul(ps[b2][:, 0:n], wb_[:, o, k, :], ybT[:, k, t0:t0 + n], start=(k == 0), stop=(k == 3))
                                    for k in range(KT):
                                        nc.tensor.matmul(ps[b3][:, 0:n], wgt[:, o, k, :], nl[:, k, 0:n], start=(k == 0), stop=(k == KT - 1))
                                    for k in range(KT):
                                        inst = nc.tensor.matmul(ps[b4][:, 0:n], wgt[:, 8 + o, k, :], nl[:, k, 0:n], start=(k == 0), stop=(k == KT - 1))
                                    return inst
                                tr.op("pe", [("wa", o), ("wb", o), ("wgt", o), ("wgt", 8 + o), nk], [("ps", b1), ("ps", b2), ("ps", b3), ("ps", b4)], mm)
                                tr.op("act", [("ps", b3)], [k3], lambda a3=a3, b3=b3, n=n: nc.scalar.activation(out=a3[:, 0:n], in_=ps[b3][:, 0:n], func=AF.Sigmoid))
                                tr.op("act", [("ps", b4)], [k4], lambda a4=a4, b4=b4, n=n: nc.scalar.activation(out=a4[:, 0:n], in_=ps[b4][:, 0:n], func=AF.Sigmoid))
                                tr.op("dve", [("ps", b1), k3], [k3], lambda a3=a3, b1=b1, n=n: nc.vector.tensor_tensor(out=a3[:, 0:n], in0=ps[b1][:, 0:n], in1=a3[:, 0:n], op=ALU.mult))
                                tr.op("dve", [("ps", b2), k4], [k4], lambda a4=a4, b2=b2, n=n: nc.vector.tensor_tensor(out=a4[:, 0:n], in0=ps[b2][:, 0:n], in1=a4[:, 0:n], op=ALU.mult))
                                tr.op("dve", [k3, k4], [("mT", o)], lambda a3=a3, a4=a4, o=o, n=n: nc.vector.tensor_tensor(out=mT[:, o, 0:n], in0=a3[:, 0:n], in1=a4[:, 0:n], op=ALU.add))
                            for o2 in range(KT):
                                b = bank()

                                def mm(o2=o2, b=b, n=n):
                                    inst = None
                                    for k in range(KT):
                                        inst = nc.tensor.matmul(ps[b][:, 0:n], wo[:, o2, k, :], mT[:, k, 0:n], start=(k == 0), stop=(k == KT - 1))
                                    return inst
                                tr.op("pe", [("wo", o2)] + [("mT", o) for o in range(KT)], [("ps", b)], mm)
                                tr.op("dve", [("ps", b), hk], [hk], lambda o2=o2, b=b, n=n, h_=h_, mc=mc: nc.vector.scalar_tensor_tensor(
                                    out=h_[:, o2, 0:n], in0=ps[b][:, 0:n], scalar=mods[l][:, 16 + o2, mc:mc + 1], in1=h_[:, o2, 0:n], op0=ALU.mult, op1=ALU.add))
                            tr.dma("sp", hdst[:, :, t0:t0 + n], h_[:, :, 0:n], [hk], [("hbuf", ti)])
                            norm_sb(nsb, h_, hk, n, lambda k, mc=mc: ms2[l][:, k, mc:mc + 1], lambda k, mc=mc: mods[l][:, 24 + k, mc:mc + 1],
                                    lambda k, n=n: nlo[:, k, 0:n], "nlo")
                            tr.dma("sp", nlbuf2[:, :, t0:t0 + n], nlo[:, :, 0:n], ["nlo"], [("nlbuf2", ti)])
                    tr.barrier()
                if debug and s == 0 and li == 0:
                    tr.dma("sp", dbg_out("hmix", [128, KT, LT], F32), hbuf[s], [], [])
                    tr.barrier()

                with contextlib.ExitStack() as sf:
                    nl2 = sb("nl2", [128, KT, LT], BF16, sf)
                    for ti, (t0, n, isc) in enumerate(TILES):
                        tr.dma("sp", nl2[:, :, t0:t0 + n], nlbuf2[:, :, t0:t0 + n], [], [("nl2", ti)])
                    for half in range(2):
                        with contextlib.ExitStack() as sh:
                            act = sb("actF", [128, 11, LT], BF16, sh)
                            wd = sb("wd", [128, 11, D], BF16, sh)
                            tr.dma("pool", wd[:], w_dn[l, :, 11 * half:11 * half + 11, :], [], ["wd"])
                            with contextlib.ExitStack() as su:
                                scope("E_up%d" % half, su)
                                wu = [sb("wu%d" % i, [128, KT, 256], BF16, su) for i in range(3)]
                                dg9 = [sb("dg9_%d" % i, [128, 9, 128], BF16, su) for i in range(2)]
                                Gc = [sb("Gc%d" % i, [128, GCTX], BF16, su) for i in range(2)]
                                Gl = [sb("Gl%d" % i, [128, GLAT], BF16, su) for i in range(2)]
                                vT = [sb("vT%d" % i, [128, LT], BF16, su) for i in range(2)]
                                tsl = [sb("tsl%d" % i, [128, 512], BF16, su) for i in range(2)]
                                for i in range(2):
                                    tr.op("pool", [], [("G", i)], lambda i=i: nc.gpsimd.memset(Gc[i][:], 0.0))
                                    tr.op("pool", [], [("G", i)], lambda i=i: nc.gpsimd.memset(Gl[i][:], 0.0))
                                pi_ = 0
                                for jj in range(11):
                                    j = 11 * half + jj
                                    w_, wk = wu[jj % 3], "wu%d" % (jj % 3)
                                    d9, dk = dg9[jj % 2], "dg9_%d" % (jj % 2)
                                    gc, gl, gk = Gc[jj % 2], Gl[jj % 2], ("G", jj % 2)
                                    v_, vk = vT[jj % 2], "vT%d" % (jj % 2)
                                    gl3 = gl[:].rearrange("p (a b) -> p a b", b=66)
                                    tr.dma("pool", w_[:], w_up[l, j], [], [wk])

                                    def mk(d9=d9, j=j):
                                        inst = None
                                        for t in range(9):
                                            inst = nc.vector.tensor_scalar(out=d9[:, t, :], in0=identf[:], scalar1=fdw_s[:, l, j, t:t + 1], scalar2=None, op0=ALU.mult)
                                        return inst
                                    tr.op("dve", [], [dk], mk)
                                    for ti, (t0, n, isc) in enumerate(TILES):
                                        bg, bv = bank(), bank()

                                        def mm(bg=bg, bv=bv, t0=t0, n=n, w_=w_):
                                            inst = None
                                            for k in range(KT):
                                                nc.tensor.matmul(ps[bg][:, 0:n], w_[:, k, 0:128], nl2[:, k, t0:t0 + n], start=(k == 0), stop=(k == KT - 1))
                                            for k in range(KT):
                                                inst = nc.tensor.matmul(ps[bv][:, 0:n], w_[:, k, 128:256], nl2[:, k, t0:t0 + n], start=(k == 0), stop=(k == KT - 1))
                                            return inst
                                        tr.op("pe", [wk, ("nl2", ti)], [("ps", bg), ("ps", bv)], mm)
                                        if isc:
                                            tr.op("act", [("ps", bg)], [gk], lambda gc=gc, bg=bg: nc.scalar.activation(out=gc[:, 1:257], in_=ps[bg][:, 0:256], func=AF.Copy))
                                        else:
                                            r0 = 1 + 8 * (ti - 1)
                                            tr.op("act", [("ps", bg)], [gk], lambda gl3=gl3, bg=bg, r0=r0: nc.scalar.activation(
                                                out=gl3[:, r0:r0 + 8, 1:65], in_=ps[bg][:, 0:512].rearrange("p (a b) -> p a b", b=64), func=AF.Copy))
                                        tr.op("dve", [("ps", bv)], [vk], lambda v_=v_, bv=bv, t0=t0, n=n: nc.vector.tensor_copy(out=v_[:, t0:t0 + n], in_=ps[bv][:, 0:n]))
                                    for ti, (t0, n, isc) in enumerate(TILES):
                                        b = bank()
                                        ts_, tk = tsl[pi_ % 2], "tsl%d" % (pi_ % 2)
                                        pi_ += 1
                                        if isc:
                                            def mm(b=b, d9=d9, gc=gc):
                                                inst = None
                                                for dx in (-1, 0, 1):
                                                    inst = nc.tensor.matmul(ps[b][:, 0:256], d9[:, 4 + dx, :], gc[:, 1 + dx:257 + dx], start=(dx == -1), stop=(dx == 1))
                                                return inst
                                        else:
                                            r0 = 1 + 8 * (ti - 1)

                                            def mm(b=b, d9=d9, gl3=gl3, r0=r0):
                                                inst = None
                                                o3 = ps[b][:, 0:512].rearrange("p (a b) -> p a b", b=64)
                                                for t in range(9):
                                                    dy, dx = t // 3 - 1, t % 3 - 1
                                                    inst = nc.tensor.matmul(o3, d9[:, t, :], gl3[:, r0 + dy:r0 + dy + 8, 1 + dx:65 + dx], start=(t == 0), stop=(t == 8))
                                                return inst
                                        tr.op("pe", [dk, gk], [("ps", b)], mm)
                                        tr.op("act", [("ps", b)], [tk], lambda ts_=ts_, b=b, n=n, j=j: nc.scalar.activation(
                                            out=ts_[:, 0:n], in_=ps[b][:, 0:n], func=AF.Silu, bias=fdb_s[:, l, j:j + 1], scale=1.0))
                                        tr.op("dve", [tk, vk], [("act", jj)], lambda ts_=ts_, v_=v_, jj=jj, t0=t0, n=n: nc.vector.tensor_tensor(
                                            out=act[:, jj, t0:t0 + n], in0=ts_[:, 0:n], in1=v_[:, t0:t0 + n], op=ALU.mult))
                            tr.barrier()
                            with contextlib.ExitStack() as sdn:
                                scope("E_dn%d" % half, sdn)
                                hT = [sb("hTe%d" % i, [128, KT, 512], F32, sdn) for i in range(2)]
                                do_final = final and l == last_layer and half == 1
                                do_next = half == 1 and li + 1 < len(layers)
                                if do_next:
                                    lnext = layers[li + 1]
                                    nsb = alloc_norm_sb(sdn)
                                    nlo = sb("nlo", [128, KT, 512], BF16, sdn)
                                if do_final:
                                    sqf = sb("sqf", [128, KT, 512], BF16, sdn)
                                    sdf = sb("sdf", [128, 512], F32, sdn)
                                    rsf = sb("rsf", [128, 512], F32, sdn)
                                    of = sb("of", [128, KT, 512], F32, sdn)
                                for ti, (t0, n, isc) in enumerate(TILES):
                                    mc = modcol(isc)
                                    h_, hk = hT[ti % 2], "hTe%d" % (ti % 2)

                                    def loadE(tj):
                                        (ta_, na_, _) = TILES[tj]
                                        tr.dma("sp", hT[tj % 2][:, :, 0:na_], hdst[:, :, ta_:ta_ + na_], [], ["hTe%d" % (tj % 2)])
                                    if ti == 0:
                                        loadE(0)
                                    if ti + 1 < len(TILES):
                                        loadE(ti + 1)
                                    for o in range(KT):
                                        b = bank()

                                        def mm(o=o, b=b, t0=t0, n=n):
                                            inst = None
                                            for jj in range(11):
                                                inst = nc.tensor.matmul(ps[b][:, 0:n], wd[:, jj, 128 * o:128 * (o + 1)], act[:, jj, t0:t0 + n], start=(jj == 0), stop=(jj == 10))
                                            return inst
                                        tr.op("pe", ["wd"], [("ps", b)], mm)
                                        tr.op("dve", [("ps", b), hk], [hk], lambda o=o, b=b, n=n, h_=h_, mc=mc: nc.vector.scalar_tensor_tensor(
                                            out=h_[:, o, 0:n], in0=ps[b][:, 0:n], scalar=mods[l][:, 40 + o, mc:mc + 1], in1=h_[:, o, 0:n], op0=ALU.mult, op1=ALU.add))
                                    if do_final:
                                        if isc:
                                            continue
                                        tr.op("act", [hk], ["sqf"], lambda h_=h_: nc.scalar.activation(out=sqf[:], in_=h_[:], func=AF.Square))
                                        b = bank()

                                        def mm(b=b):
                                            inst = None
                                            for k in range(KT):
                                                inst = nc.tensor.matmul(ps[b][:, 0:512], onesb[:], sqf[:, k, :], start=(k == 0), stop=(k == KT - 1))
                                            return inst
                                        tr.op("pe", ["sqf"], [("ps", b)], mm)
                                        tr.op("act", [("ps", b)], ["sdf"], lambda b=b: nc.scalar.activation(out=sdf[:], in_=ps[b][:, 0:512], func=AF.Sqrt, bias=epsc[:, 0:1], scale=1.0 / D))
                                        tr.op("dve", ["sdf"], ["rsf"], lambda: nc.vector.reciprocal(out=rsf[:], in_=sdf[:]))
                                        tr.op("dve", [hk, "rsf"], ["of"], lambda h_=h_: nc.vector.tensor_tensor(
                                            out=of[:], in0=h_[:], in1=rsf[:].unsqueeze(1).to_broadcast([128, KT, 512]), op=ALU.mult))
                                        tr.op("dve", ["of"], ["of"], lambda: nc.vector.tensor_tensor(
                                            out=of[:], in0=of[:], in1=fing_s[:].unsqueeze(2).to_broadcast([128, KT, 512]), op=ALU.mult))
                                        tr.dma("sp", out_d[s, :, :, t0 - LCTX:t0 - LCTX + n], of[:], ["of"], [])
                                        if not fused:
                                            tr.dma("sp", hdst[:, :, t0:t0 + n], h_[:, :, 0:n], [hk], [("hbuf", ti)])
                                    else:
                                        tr.dma("sp", hdst[:, :, t0:t0 + n], h_[:, :, 0:n], [hk], [("hbuf", ti)])
                                        if do_next:
                                            norm_sb(nsb, h_, hk, n, lambda k, mc=mc: ms1[lnext][:, k, mc:mc + 1], lambda k, mc=mc: mods[lnext][:, k, mc:mc + 1],
                                                    lambda k, n=n: nlo[:, k, 0:n], "nlo")
                                            tr.dma("sp", nlbuf[:, :, t0:t0 + n], nlo[:, :, 0:n], ["nlo"], [("nlbuf", ti)])
                            tr.barrier()
        tr.finish("sp")
    return nc, dbg


def _kt(w):
    K, O = w.shape
    return np.ascontiguousarray(w.reshape(K // 128, 128, O).transpose(1, 0, 2))


def _vec(v, n):
    return np.ascontiguousarray(v.reshape(n, 128).T)


def prep_weights(inp):
    f = lambda a: np.asarray(a, dtype=np.float32)
    W = {}
    W["ada_w"] = np.stack([_kt(f(inp["ada_w"][l])) for l in range(DEPTH)])
    W["ada_b"] = np.stack([_vec(f(inp["ada_b"][l]), 48) for l in range(DEPTH)])
    W["n1g"] = np.ascontiguousarray(np.stack([_vec(f(inp["norm1_g"][l]), KT) for l in range(DEPTH)], axis=1))
    W["n2g"] = np.ascontiguousarray(np.stack([_vec(f(inp["norm2_g"][l]), KT) for l in range(DEPTH)], axis=1))
    W["fing"] = _vec(f(inp["final_g"]), KT)
    win = [_kt(f(inp["w_in"][l])) for l in range(DEPTH)]
    def om(w):
        p, k, o = w.shape
        return np.ascontiguousarray(w.reshape(p, k, o // 128, 128).transpose(2, 0, 1, 3))
    W["w_inA"] = np.stack([om(w[:, :, :1536]) for w in win])
    W["w_inG"] = np.stack([om(w[:, :, 1536:]) for w in win])
    W["w_glu"] = np.stack([_kt(f(inp["w_glu"][l])) for l in range(DEPTH)])
    W["w_a"] = np.stack([om(_kt(f(inp["w_a"][l]))) for l in range(DEPTH)])
    W["w_b"] = np.stack([om(_kt(f(inp["w_b"][l]))) for l in range(DEPTH)])
    W["w_out"] = np.stack([om(_kt(f(inp["w_out"][l]))) for l in range(DEPTH)])
    up = np.stack([_kt(f(inp["ffn_w_up"][l])) for l in range(DEPTH)])
    g = up[:, :, :, :2816].reshape(DEPTH, 128, KT, NH, 128)
    v = up[:, :, :, 2816:].reshape(DEPTH, 128, KT, NH, 128)
    W["w_up"] = np.ascontiguousarray(np.concatenate([g, v], axis=-1).transpose(0, 3, 1, 2, 4))
    W["w_dn"] = np.stack([_kt(f(inp["ffn_w_down"][l])) for l in range(DEPTH)])
    W["cdw"] = np.ascontiguousarray(f(inp["conv_dw"]).reshape(DEPTH, 31, 4, 128).transpose(3, 0, 2, 1))
    for nm, key in [("cdb", "conv_dw_b"), ("clg", "conv_ln_g"), ("clb", "conv_ln_b")]:
        W[nm] = np.ascontiguousarray(f(inp[key]).reshape(DEPTH, 4, 128).transpose(2, 0, 1))
    W["fdw"] = np.ascontiguousarray(f(inp["ffn_dw"]).reshape(DEPTH, 9, NH, 128).transpose(3, 0, 2, 1))
    W["fdb"] = np.ascontiguousarray(f(inp["ffn_dw_b"]).reshape(DEPTH, NH, 128).transpose(2, 0, 1))

    def s5lay(a):
        L = a.shape[0]
        rest = a.shape[4:]
        a = a.reshape(L, 2, 16, 2, 64, *rest)
        a = np.moveaxis(a, [3, 4, 1, 2], [1, 2, 3, 4])
        return np.ascontiguousarray(a.reshape(L, 128, 32, *rest))
    W["lamre"] = s5lay(f(inp["s5_lam_re"]))
    W["lamim"] = s5lay(f(inp["s5_lam_im"]))
    W["logdt"] = s5lay(np.broadcast_to(f(inp["s5_log_dt"])[:, :, :, None], (DEPTH, 2, 32, 64)).copy())
    W["bre"] = s5lay(f(inp["s5_b_re"]))
    W["bim"] = s5lay(f(inp["s5_b_im"]))
    W["cre"] = s5lay(np.swapaxes(f(inp["s5_c_re"]), 3, 4))
    W["cim"] = s5lay(np.swapaxes(f(inp["s5_c_im"]), 3, 4))
    d = f(inp["s5_d"]).reshape(DEPTH, 32, 16)
    W["s5d"] = np.ascontiguousarray(np.broadcast_to(d.transpose(0, 2, 1)[:, None, :, :], (DEPTH, 8, 16, 32)).reshape(DEPTH, 128, 32))
    W["ident"] = np.eye(128, dtype=np.float32)
    sel1 = np.zeros((128, 64, 128), np.float32)
    sel2 = np.zeros((128, 64, 128), np.float32)
    for gq in range(8):
        for r in range(8):
            for h in range(16):
                sel1[16 * gq + h, 8 * gq + r, 16 * r + h] = 1.0
                sel2[16 * r + h, 8 * r + gq, 16 * gq + h] = 1.0
    W["sel1"] = sel1
    W["sel2"] = sel2
    ev = np.zeros((128, 8, 32), np.float32)
    for r in range(8):
        ev[:, r, :16] = r + 1
        ev[:, r, 16:] = 8 - r
    W["evals"] = ev
    nm = np.zeros((128, 2, 128), np.float32)
    rin = np.arange(128)[:, None] // 16
    rout = np.arange(128)[None, :] // 16
    nm[:, 0, :] = -(rin > rout).astype(np.float32)
    nm[:, 1, :] = -(rin < rout).astype(np.float32)
    W["negmask"] = nm
    W["jvals"] = np.ascontiguousarray(np.broadcast_to((8.0 * np.arange(1, 17, dtype=np.float32))[None, :, None], (128, 16, 32)))
    return W


def prep_x(x, ctx, c, c_ctx, nseq_per_core, ncores):
    maps = []
    for ci in range(ncores):
        xs, cs = [], []
        for j in range(nseq_per_core):
            b = ci * nseq_per_core + j
            full = np.concatenate([ctx[b], x[b]], axis=0)
            xs.append(full.T.reshape(KT, 128, LT).transpose(1, 0, 2))
            cs.append(c[b])
        cs = cs + [c[0]] * (2 - len(cs))
        cv = np.stack([_vec(cs[0], KT), _vec(cs[1], KT), _vec(c_ctx, KT)], axis=-1)
        maps.append({"xT": np.ascontiguousarray(np.stack(xs)), "cvec": np.ascontiguousarray(cv)})
    return maps


FUSED = True
_CACHE = {}


def kernel(**inputs):
    f = lambda a: np.asarray(a, dtype=np.float32)
    W = prep_weights(inputs)
    B = inputs["x"].shape[0]
    ncores = 8
    nseq = B // ncores
    maps = prep_x(f(inputs["x"]), f(inputs["ctx"]), f(inputs["c"]), f(inputs["c_ctx"]), nseq, ncores)
    groups = [list(range(DEPTH))] if FUSED else [[l] for l in range(DEPTH)]
    out = None
    for gi, layers in enumerate(groups):
        fused = len(groups) == 1
        nc, _ = build_program(layers, nseq=nseq, fused=fused)
        in_maps = [dict(W, **m) for m in maps]
        res = run_bass_kernel_spmd(nc, in_maps, core_ids=list(range(ncores)))
        if not fused and gi < len(groups) - 1:
            for ci in range(ncores):
                maps[ci]["xT"] = np.ascontiguousarray(res.results[ci]["hout"])
        if DEPTH - 1 in layers:
            outs = []
            for ci in range(ncores):
                o = res.results[ci]["out"]
                outs.append(o.transpose(0, 3, 2, 1).reshape(nseq, LLAT, D))
            out = np.concatenate(outs, axis=0).astype(np.float32)
    return out
```

```python
import contextlib
import math
import numpy as np
import concourse.bass as bass
import concourse.mybir as mybir
from concourse.bass_utils import run_bass_kernel_spmd

F32 = mybir.dt.float32
BF16 = mybir.dt.bfloat16
I32 = mybir.dt.int32
AF = mybir.ActivationFunctionType
ALU = mybir.AluOpType

D = 1024
KT = 8
LCTX = 256
LLAT = 2048
LT = LCTX + LLAT
NQ = LT // 8
QC = LCTX // 8
DEPTH = 4
NH = 22
EPS = 1e-6
TILES = [(0, 256, True), (256, 512, False), (768, 512, False), (1280, 512, False), (1792, 512, False)]
APAD = 2352
GCTX = 258
GLAT = 34 * 66
TWO_PI = 2.0 * math.pi


def apos(t0):
    return t0 + 16 if t0 < LCTX else t0 + 32


class Tracker:
    def __init__(self, nc, es, ndsem=48):
        self.nc = nc
        self.eng = {"pe": nc.tensor, "act": nc.scalar, "dve": nc.vector, "pool": nc.gpsimd, "sp": nc.sync}
        self.sem = {e: es.enter_context(nc.semaphore("sem_" + e)) for e in self.eng}
        self.cnt = {e: 0 for e in self.eng}
        self.dsem = [es.enter_context(nc.semaphore("dsem%d" % i)) for i in range(ndsem)]
        self.dcnt = [0] * ndsem
        self.dpool = {"sp": list(range(0, 20)), "pool": list(range(20, ndsem))}
        self.dnext = {"sp": 0, "pool": 0}
        self.waited = {e: {} for e in self.eng}
        self.res = {}

    def _semh(self, key):
        return self.sem[key[1]] if key[0] == "e" else self.dsem[key[1]]

    def _collect(self, e, reads, writes, is_dma):
        need = {}

        def add(dep, kind):
            semkey, val, src = dep
            if (not is_dma) and src == e and (e == "pe" or kind != "raw"):
                return
            if need.get(semkey, 0) < val:
                need[semkey] = val

        for k in reads:
            r = self.res.get(k)
            if r is not None and r["w"] is not None:
                add(r["w"], "raw")
        for k in writes:
            r = self.res.get(k)
            if r is not None:
                if r["w"] is not None:
                    add(r["w"], "waw")
                for dep in r["r"].values():
                    add(dep, "war")
        return need

    def _wait(self, e, need):
        for semkey, val in need.items():
            if self.waited[e].get(semkey, 0) < val:
                self.eng[e].wait_ge(self._semh(semkey), val)
                self.waited[e][semkey] = val

    def _record(self, dep, reads, writes):
        for k in writes:
            self.res[k] = {"w": dep, "r": {}}
        for k in reads:
            r = self.res.setdefault(k, {"w": None, "r": {}})
            r["r"][dep[0]] = dep

    def op(self, e, reads, writes, fn):
        self._wait(e, self._collect(e, reads, writes, False))
        inst = fn()
        self.cnt[e] += 1
        inst.then_inc(self.sem[e], 1)
        self._record((("e", e), self.cnt[e], e), reads, writes)

    def dma(self, q, out, in_, reads, writes):
        need = self._collect(q, reads, writes, True)
        pool_ = self.dpool[q]
        s = pool_[self.dnext[q]]
        self.dnext[q] = (self.dnext[q] + 1) % len(pool_)
        if self.dcnt[s] > 0:
            need[("d", s)] = max(need.get(("d", s), 0), self.dcnt[s])
        self._wait(q, need)
        inst = self.eng[q].dma_start(out=out, in_=in_)
        self.dcnt[s] += 16
        inst.then_inc(self.dsem[s], 16)
        self._record((("d", s), self.dcnt[s], "dma"), reads, writes)

    def barrier(self):
        for e in self.eng:
            need = {}
            for e2 in self.eng:
                if e2 != e and self.cnt[e2] > 0:
                    need[("e", e2)] = self.cnt[e2]
            for s in range(len(self.dsem)):
                if self.dcnt[s] > 0:
                    need[("d", s)] = self.dcnt[s]
            self._wait(e, need)
        self.res = {}

    def finish(self, e="sp"):
        need = {}
        for e2 in self.eng:
            if e2 != e and self.cnt[e2] > 0:
                need[("e", e2)] = self.cnt[e2]
        for s in range(len(self.dsem)):
            if self.dcnt[s] > 0:
                need[("d", s)] = self.dcnt[s]
        self._wait(e, need)


def build_program(layers, nseq=2, fused=True, debug=False, prof=False):
    nc = bass.Bass("TRN2", target_bir_lowering=False)
    last_layer = DEPTH - 1
    final = last_layer in layers

    def din(name, shape, dt=F32):
        return nc.dram_tensor(name, list(shape), dt, kind="ExternalInput").ap()

    xT = din("xT", [nseq, 128, KT, LT])
    cvec = din("cvec", [128, KT, 3])
    ada_w = din("ada_w", [DEPTH, 128, KT, 6 * D])
    ada_b = din("ada_b", [DEPTH, 128, 48])
    n1g = din("n1g", [128, DEPTH, KT])
    n2g = din("n2g", [128, DEPTH, KT])
    fing = din("fing", [128, KT])
    w_inA = din("w_inA", [DEPTH, 12, 128, KT, 128])
    w_inG = din("w_inG", [DEPTH, 16, 128, KT, 128])
    w_glu = din("w_glu", [DEPTH, 128, 4, 512])
    w_a = din("w_a", [DEPTH, 8, 128, 4, 128])
    w_b = din("w_b", [DEPTH, 8, 128, 4, 128])
    w_out = din("w_out", [DEPTH, 8, 128, KT, 128])
    w_up = din("w_up", [DEPTH, NH, 128, KT, 256])
    w_dn = din("w_dn", [DEPTH, 128, NH, D])
    cdw = din("cdw", [128, DEPTH, 4, 31])
    cdb = din("cdb", [128, DEPTH, 4])
    clg = din("clg", [128, DEPTH, 4])
    clb = din("clb", [128, DEPTH, 4])
    fdw = din("fdw", [128, DEPTH, NH, 9])
    fdb = din("fdb", [128, DEPTH, NH])
    lamre = din("lamre", [DEPTH, 128, 32])
    lamim = din("lamim", [DEPTH, 128, 32])
    logdt = din("logdt", [DEPTH, 128, 32])
    bre = din("bre", [DEPTH, 128, 32, 16])
    bim = din("bim", [DEPTH, 128, 32, 16])
    cre = din("cre", [DEPTH, 128, 32, 16])
    cim = din("cim", [DEPTH, 128, 32, 16])
    s5d = din("s5d", [DEPTH, 128, 32])
    ident_d = din("ident", [128, 128])
    sel1_d = din("sel1", [128, 64, 128])
    sel2_d = din("sel2", [128, 64, 128])
    evals_d = din("evals", [128, 8, 32])
    negmask_d = din("negmask", [128, 2, 128])
    jvals_d = din("jvals", [128, 16, 32])

    if final:
        out_d = nc.dram_tensor("out", [nseq, 128, KT, LLAT], F32, kind="ExternalOutput").ap()
    if fused:
        hbuf = nc.dram_tensor("hbuf", [nseq, 128, KT, LT], F32).ap()
    else:
        hbuf = nc.dram_tensor("hout", [nseq, 128, KT, LT], F32, kind="ExternalOutput").ap()
    nlbuf = nc.dram_tensor("nlbuf", [128, KT, LT], BF16).ap()
    nlbuf2 = nc.dram_tensor("nlbuf2", [128, KT, LT], BF16).ap()
    dbg = {}

    def dbg_out(name, shape, dt=F32):
        dbg[name] = nc.dram_tensor("dbg_" + name, list(shape), dt, kind="ExternalOutput").ap()
        return dbg[name]

    es = contextlib.ExitStack()
    with es:
        tr = Tracker(nc, es)
        E = es.enter_context

        uid = [0]

        def scope(name, stack):
            if prof:
                stack.enter_context(nc.named_scope(name))

        def sb(name, shape, dt, stack=None):
            uid[0] += 1
            return (stack or es).enter_context(nc.sbuf_tensor("s%d_%s" % (uid[0], name), list(shape), dt))

        ps = [E(nc.psum_tensor("ps%d" % i, [128, 512], F32)) for i in range(7)]
        psb = E(nc.psum_tensor("psb", [128, 1024], BF16))
        ps_i = [0]

        def bank():
            i = ps_i[0]
            ps_i[0] = (i + 1) % 7
            return i

        identf = sb("identf", [128, 128], F32)
        identb = sb("identb", [128, 128], BF16)
        onesb = sb("onesb", [128, 128], BF16)
        o512b = sb("o512b", [128, 128], BF16)
        epsc = sb("epsc", [128, 1], F32)
        negmask = sb("negmask", [128, 2, 128], F32)
        evals = sb("evals", [128, 8, 32], F32)
        n1g_s = sb("n1g_s", [128, DEPTH, KT], F32)
        n2g_s = sb("n2g_s", [128, DEPTH, KT], F32)
        fing_s = sb("fing_s", [128, KT], F32)
        cdw_s = sb("cdw_s", [128, DEPTH, 4, 31], F32)
        cdb_s = sb("cdb_s", [128, DEPTH, 4], F32)
        clg_s = sb("clg_s", [128, DEPTH, 4], F32)
        clb_s = sb("clb_s", [128, DEPTH, 4], F32)
        fdw_s = sb("fdw_s", [128, DEPTH, NH, 9], F32)
        fdb_s = sb("fdb_s", [128, DEPTH, NH], F32)
        mods = {l: sb("mods%d" % l, [128, 48, 3], F32) for l in layers}
        ms1 = {l: sb("ms1_%d" % l, [128, KT, 3], F32) for l in layers}
        ms2 = {l: sb("ms2_%d" % l, [128, KT, 3], F32) for l in layers}

        for (dst, src, key) in [(identf, ident_d, "identf"), (negmask, negmask_d, "negmask"), (evals, evals_d, "evals"),
                                (n1g_s, n1g, "n1g"), (n2g_s, n2g, "n2g"), (fing_s, fing, "fing"), (cdw_s, cdw, "cdw"),
                                (cdb_s, cdb, "cdb"), (clg_s, clg, "clg"), (clb_s, clb, "clb"), (fdw_s, fdw, "fdw"),
                                (fdb_s, fdb, "fdb")]:
            tr.dma("sp", dst[:], src, [], [key])
        tr.dma("pool", identb[:], ident_d, [], ["identb"])
        tr.op("dve", [], ["onesb"], lambda: nc.vector.memset(onesb[:], 1.0))
        tr.op("dve", [], ["o512b"], lambda: nc.vector.memset(o512b[:], 1.0 / 512.0))
        tr.op("dve", [], ["epsc"], lambda: nc.vector.memset(epsc[:], EPS))

        def phase_mods():
            with contextlib.ExitStack() as st:
                cv = sb("cv", [128, KT, 3], F32, st)
                scb = sb("scb", [128, KT, 3], BF16, st)
                adab = sb("adab", [128, 48], F32, st)
                wch = [sb("wch%d" % i, [128, KT, 1024], BF16, st) for i in range(2)]
                tr.dma("sp", cv[:], cvec, [], ["cv"])
                tr.op("act", ["cv"], ["scb"], lambda: nc.scalar.activation(out=scb[:], in_=cv[:], func=AF.Silu))
                ci = 0
                for l in layers:
                    tr.dma("sp", adab[:], ada_b[l], [], ["adab"])
                    for j in range(6):
                        w = wch[ci % 2]
                        wk = "wch%d" % (ci % 2)
                        ci += 1
                        tr.dma("pool", w[:], ada_w[l, :, :, 1024 * j:1024 * (j + 1)], [], [wk])
                        b = bank()

                        def mm(w=w, b=b):
                            inst = None
                            for ot in range(8):
                                for k in range(KT):
                                    inst = nc.tensor.matmul(ps[b][:, 3 * ot:3 * ot + 3], w[:, k, 128 * ot:128 * (ot + 1)],
                                                            scb[:, k, :], start=(k == 0), stop=(k == KT - 1))
                            return inst
                        tr.op("pe", [wk, "scb"], [("ps", b)], mm)
                        tr.op("dve", [("ps", b), "adab"], [("mods", l)],
                              lambda l=l, j=j, b=b: nc.vector.tensor_tensor(
                                  out=mods[l][:, 8 * j:8 * j + 8, :],
                                  in0=ps[b][:, 0:24].rearrange("p (a c) -> p a c", c=3),
                                  in1=adab[:, 8 * j:8 * j + 8].unsqueeze(2).to_broadcast([128, 8, 3]), op=ALU.add))
                    for (ms, gs, gk, o0) in [(ms1, n1g_s, "n1g", 8), (ms2, n2g_s, "n2g", 32)]:
                        tr.op("dve", [("mods", l)], [("ms", l, o0)],
                              lambda ms=ms, o0=o0, l=l: nc.vector.tensor_scalar(
                                  out=ms[l][:], in0=mods[l][:, o0:o0 + 8, :], scalar1=1.0, scalar2=None, op0=ALU.add))
                        tr.op("dve", [("ms", l, o0), gk], [("ms", l, o0)],
                              lambda ms=ms, gs=gs, l=l: nc.vector.tensor_tensor(
                                  out=ms[l][:], in0=ms[l][:],
                                  in1=gs[:, l, :].unsqueeze(2).to_broadcast([128, KT, 3]), op=ALU.mult))
                tr.barrier()

        phase_mods()

        W1d = nc.dram_tensor("W1d", [DEPTH, 128, 32, 2, 128], BF16).ap()
        W3d = nc.dram_tensor("W3d", [DEPTH, 128, 2, 32, 128], BF16).ap()
        W4d = nc.dram_tensor("W4d", [DEPTH, 128, 32, 128], BF16).ap()
        AP1d = nc.dram_tensor("AP1d", [DEPTH, 128, 17, 2, 32], F32).ap()
        AP2d = nc.dram_tensor("AP2d", [DEPTH, 128, 17, 2, 32], F32).ap()

        def s5_gen(l):
            with contextlib.ExitStack() as sg_:
                W3 = sb("W3", [128, 2, 32, 128], BF16, sg_)
                W4 = sb("W4", [128, 32, 128], BF16, sg_)
                AP1 = sb("AP1", [128, 17, 2, 32], F32, sg_)
                AP2 = sb("AP2", [128, 17, 2, 32], F32, sg_)
                tr.op("dve", [], ["AP1"], lambda: nc.vector.memset(AP1[:, 0, :, :], 1.0))
                tr.op("dve", [], ["AP2"], lambda: nc.vector.memset(AP2[:, 0, :, :], 0.0))
                scope("C_gen", sg_)
                def t32(name, shape=(128, 32)):
                    return sb(name, list(shape), F32, sg_)
                lr = t32("lr"); li_ = t32("li"); ldt = t32("ldt"); dsel = t32("dsel")
                br = t32("br", (128, 32, 16)); bi = t32("bi", (128, 32, 16))
                cr = t32("cr", (128, 32, 16)); ci_ = t32("ci", (128, 32, 16))
                for (dst, src, key) in [(lr, lamre, "lr"), (li_, lamim, "li"), (ldt, logdt, "ldt"), (dsel, s5d, "dsel"),
                                        (br, bre, "br"), (bi, bim, "bi"), (cr, cre, "cr"), (ci_, cim, "ci")]:
                    tr.dma("sp", dst[:], src[l], [], [key])
                dt_ = t32("dt"); xr = t32("xr"); th = t32("th"); mag = t32("mag")
                cs = t32("cs"); sn = t32("sn"); abr = t32("abr"); abi = t32("abi")
                t1 = t32("t1"); t2 = t32("t2"); t3 = t32("t3"); kre = t32("kre"); kim = t32("kim")
                bbr = t32("bbr", (128, 32, 16)); bbi = t32("bbi", (128, 32, 16)); tb = t32("tb", (128, 32, 16))
                ex = t32("ex", (128, 8, 32)); eth = t32("eth", (128, 8, 32))
                pmag = t32("pmag", (128, 8, 32)); nmag = t32("nmag", (128, 8, 32))
                psn = t32("psn", (128, 8, 32)); pcs = t32("pcs", (128, 8, 32))
                pwr = t32("pwr", (128, 8, 32)); pwi = t32("pwi", (128, 8, 32))
                nwr = t32("nwr", (128, 8, 32)); nwi = t32("nwi", (128, 8, 32))
                spw = contextlib.ExitStack()
                rt1 = sb("rt1", [128, 512], F32, spw); rti = sb("rti", [128, 512], I32, spw); rt2 = sb("rt2", [128, 512], F32, spw)
                V = nc.vector

                def dv(reads, writes, fn):
                    tr.op("dve", reads, writes, fn)

                def sin_of(arg, argk, out, outk, shift, nel):
                    a1 = rt1[:, 0:nel]; ai = rti[:, 0:nel]; a2 = rt2[:, 0:nel]
                    dv([argk], ["rt1"], lambda: V.tensor_scalar(out=a1, in0=arg, scalar1=shift, scalar2=1.0 / TWO_PI, op0=ALU.add, op1=ALU.mult))
                    dv(["rt1"], ["rti"], lambda: V.tensor_copy(out=ai, in_=a1))
                    dv(["rti"], ["rt2"], lambda: V.tensor_copy(out=a2, in_=ai))
                    dv(["rt2", argk], ["rt1"], lambda: V.scalar_tensor_tensor(out=a1, in0=a2, scalar=-TWO_PI, in1=arg, op0=ALU.mult, op1=ALU.add))
                    dv(["rt1"], ["rt2"], lambda: V.tensor_scalar(out=a2, in0=a1, scalar1=shift, scalar2=math.pi, op0=ALU.add, op1=ALU.min))
                    dv(["rt2"], ["rt1"], lambda: V.tensor_scalar(out=a1, in0=a2, scalar1=-math.pi, scalar2=None, op0=ALU.max))
                    tr.op("act", ["rt1"], [outk], lambda: nc.scalar.activation(out=out, in_=a1, func=AF.Sin))

                def tt(out, outk, a, ak, b, bk, op):
                    dv([ak, bk], [outk], lambda: V.tensor_tensor(out=out, in0=a, in1=b, op=op))

                tr.op("act", ["ldt"], ["dt"], lambda: nc.scalar.activation(out=dt_[:], in_=ldt[:], func=AF.Exp))
                tt(xr[:], "xr", dt_[:], "dt", lr[:], "lr", ALU.mult)
                tt(th[:], "th", dt_[:], "dt", li_[:], "li", ALU.mult)
                tr.op("act", ["xr"], ["mag"], lambda: nc.scalar.activation(out=mag[:], in_=xr[:], func=AF.Exp))
                sin_of(th[:], "th", sn[:], "sn", 0.0, 32)
                sin_of(th[:], "th", cs[:], "cs", math.pi / 2, 32)
                tt(abr[:], "abr", mag[:], "mag", cs[:], "cs", ALU.mult)
                tt(abi[:], "abi", mag[:], "mag", sn[:], "sn", ALU.mult)
                tt(t1[:], "t1", lr[:], "lr", lr[:], "lr", ALU.mult)
                tt(t2[:], "t2", li_[:], "li", li_[:], "li", ALU.mult)
                tt(t1[:], "t1", t1[:], "t1", t2[:], "t2", ALU.add)
                dv(["t1"], ["t3"], lambda: V.reciprocal(out=t3[:], in_=t1[:]))
                dv(["abr"], ["t1"], lambda: V.tensor_scalar(out=t1[:], in0=abr[:], scalar1=-1.0, scalar2=None, op0=ALU.add))
                tt(t2[:], "t2", t1[:], "t1", lr[:], "lr", ALU.mult)
                tt(kre[:], "kre", abi[:], "abi", li_[:], "li", ALU.mult)
                tt(kre[:], "kre", kre[:], "kre", t2[:], "t2", ALU.add)
                tt(kre[:], "kre", kre[:], "kre", t3[:], "t3", ALU.mult)
                tt(t2[:], "t2", t1[:], "t1", li_[:], "li", ALU.mult)
                tt(kim[:], "kim", abi[:], "abi", lr[:], "lr", ALU.mult)
                tt(kim[:], "kim", kim[:], "kim", t2[:], "t2", ALU.subtract)
                tt(kim[:], "kim", kim[:], "kim", t3[:], "t3", ALU.mult)
                kreb = kre[:].unsqueeze(2).to_broadcast([128, 32, 16])
                kimb = kim[:].unsqueeze(2).to_broadcast([128, 32, 16])
                tt(bbr[:], "bbr", br[:], "br", kreb, "kre", ALU.mult)
                tt(tb[:], "tb", bi[:], "bi", kimb, "kim", ALU.mult)
                tt(bbr[:], "bbr", bbr[:], "bbr", tb[:], "tb", ALU.subtract)
                tt(bbi[:], "bbi", bi[:], "bi", kreb, "kre", ALU.mult)
                tt(tb[:], "tb", br[:], "br", kimb, "kim", ALU.mult)
                tt(bbi[:], "bbi", bbi[:], "bbi", tb[:], "tb", ALU.add)
                tt(ex[:], "ex", evals[:], "evals", xr[:].unsqueeze(1).to_broadcast([128, 8, 32]), "xr", ALU.mult)
                tt(eth[:], "eth", evals[:], "evals", th[:].unsqueeze(1).to_broadcast([128, 8, 32]), "th", ALU.mult)
                tr.op("act", ["ex"], ["pmag"], lambda: nc.scalar.activation(out=pmag[:], in_=ex[:], func=AF.Exp))
                tr.op("act", ["ex"], ["nmag"], lambda: nc.scalar.activation(out=nmag[:], in_=ex[:], func=AF.Exp, scale=-1.0))
                fl = lambda t: t[:].rearrange("p a b -> p (a b)")
                sin_of(fl(eth), "eth", fl(psn), "psn", 0.0, 256)
                sin_of(fl(eth), "eth", fl(pcs), "pcs", math.pi / 2, 256)
                tt(pwr[:], "pwr", pmag[:], "pmag", pcs[:], "pcs", ALU.mult)
                tt(pwi[:], "pwi", pmag[:], "pmag", psn[:], "psn", ALU.mult)
                tt(nwr[:], "nwr", nmag[:], "nmag", pcs[:], "pcs", ALU.mult)
                dv(["nmag", "psn"], ["nwi"], lambda: V.scalar_tensor_tensor(out=nwi[:], in0=nmag[:], scalar=-1.0, in1=psn[:], op0=ALU.mult, op1=ALU.mult))
                t16 = lambda nm: sb(nm, [128, 16, 32], F32, spw)
                jv = t16("jv")
                tr.dma("sp", jv[:], jvals_d, [], ["jv"])
                ex16 = t16("ex16"); eth16 = t16("eth16"); pm16 = t16("pm16")
                sn16 = t16("sn16"); cs16 = t16("cs16")
                tt(ex16[:], "ex16", jv[:], "jv", xr[:].unsqueeze(1).to_broadcast([128, 16, 32]), "xr", ALU.mult)
                tt(eth16[:], "eth16", jv[:], "jv", th[:].unsqueeze(1).to_broadcast([128, 16, 32]), "th", ALU.mult)
                tr.op("act", ["ex16"], ["pm16"], lambda: nc.scalar.activation(out=pm16[:], in_=ex16[:], func=AF.Exp))
                sin_of(fl(eth16), "eth16", fl(sn16), "sn16", 0.0, 512)
                sin_of(fl(eth16), "eth16", fl(cs16), "cs16", math.pi / 2, 512)
                tt(cs16[:], "cs16", cs16[:], "cs16", pm16[:], "pm16", ALU.mult)
                tt(sn16[:], "sn16", sn16[:], "sn16", pm16[:], "pm16", ALU.mult)
                dv(["cs16"], ["AP1"], lambda: V.tensor_copy(out=AP1[:, 1:17, 0, :], in_=cs16[:]))
                dv(["cs16"], ["AP1"], lambda: V.tensor_copy(out=AP1[:, 1:17, 1, :], in_=cs16[:]))
                dv(["sn16"], ["AP2"], lambda: V.tensor_copy(out=AP2[:, 1:17, 1, :], in_=sn16[:]))
                dv(["sn16"], ["AP2"], lambda: V.tensor_scalar(out=AP2[:, 1:17, 0, :], in0=sn16[:], scalar1=-1.0, scalar2=None, op0=ALU.mult))
                tr.barrier()
                spw.close()
                BN = sb("BN", [128, 2, 32, 128], BF16, sg_)
                P1 = t32("P1", (128, 16, 8, 16)); P2 = t32("P2", (128, 16, 8, 16))
                W1 = sb("W1", [128, 32, 2, 128], BF16, sg_)
                w4a = t32("w4a", (128, 128)); w4b = t32("w4b", (128, 128))
                for dd in range(2):
                    tl = slice(16 * dd, 16 * dd + 16)

                    def pw(t):
                        return t[:, :, tl].rearrange("p r t -> p t r").unsqueeze(3).to_broadcast([128, 16, 8, 16])

                    def bc(t):
                        return t[:, tl, :].unsqueeze(2).to_broadcast([128, 16, 8, 16])

                    def outv(t, pl):
                        return t[:, pl, tl, :].rearrange("p t (r h) -> p t r h", h=16)
                    tt(P1[:], "P1", pw(nwr), "nwr", bc(bbr), "bbr", ALU.mult)
                    tt(P2[:], "P2", pw(nwi), "nwi", bc(bbi), "bbi", ALU.mult)
                    tt(outv(BN, 0), ("BN", dd), P1[:], "P1", P2[:], "P2", ALU.subtract)
                    tt(P1[:], "P1", pw(nwr), "nwr", bc(bbi), "bbi", ALU.mult)
                    tt(P2[:], "P2", pw(nwi), "nwi", bc(bbr), "bbr", ALU.mult)
                    tt(outv(BN, 1), ("BN", dd), P1[:], "P1", P2[:], "P2", ALU.add)
                    tt(P1[:], "P1", pw(pwr), "pwr", bc(cr), "cr", ALU.mult)
                    tt(P2[:], "P2", pw(pwi), "pwi", bc(ci_), "ci", ALU.mult)
                    tt(outv(W3, 0), ("W3", dd), P1[:], "P1", P2[:], "P2", ALU.subtract)
                    tt(P1[:], "P1", pw(pwr), "pwr", bc(ci_), "ci", ALU.mult)
                    tt(P2[:], "P2", pw(pwi), "pwi", bc(cr), "cr", ALU.mult)
                    dv(["P1", "P2"], [("W3", dd)], lambda: V.scalar_tensor_tensor(out=outv(W3, 1), in0=P1[:], scalar=-1.0, in1=P2[:], op0=ALU.mult, op1=ALU.subtract))
                for grp in range(8):
                    def trp(grp=grp):
                        inst = None
                        for j in range(8):
                            idx = grp * 8 + j
                            tau, pl = idx // 2, idx % 2
                            inst = nc.tensor.transpose(psb[:, 128 * j:128 * (j + 1)], BN[:, pl, tau, :], identb[:])
                        return inst
                    tr.op("pe", [("BN", 0), ("BN", 1), "identb"], ["psb"], trp)
                    tr.op("act", ["psb"], [("W1", grp)],
                          lambda grp=grp: nc.scalar.activation(out=W1[:, 4 * grp:4 * grp + 4, :, :].rearrange("p t c m -> p (t c m)"), in_=psb[:, :], func=AF.Copy))
                for g in range(32):
                    gp, pair = g % 2, g // 2
                    R = slice(64 * gp, 64 * gp + 64)
                    bf_, br_ = bank(), bank()

                    def mm(bf_=bf_, br_=br_, R=R, pair=pair):
                        inst = None
                        for (bb, tau) in [(bf_, pair), (br_, 16 + pair)]:
                            nc.tensor.matmul(ps[bb][:, 0:128], BN[R, 0, tau, :], W3[R, 0, tau, :], start=True, stop=False)
                            inst = nc.tensor.matmul(ps[bb][:, 0:128], BN[R, 1, tau, :], W3[R, 1, tau, :], start=False, stop=True)
                        return inst
                    tr.op("pe", [("BN", 0), ("BN", 1), ("W3", 0), ("W3", 1)], [("ps", bf_), ("ps", br_)], mm)
                    dv([("ps", bf_), "negmask"], ["w4a"], lambda bf_=bf_: V.tensor_tensor(out=w4a[:], in0=ps[bf_][:, 0:128], in1=negmask[:, 0, :], op=ALU.mult))
                    dv([("ps", br_), "negmask"], ["w4b"], lambda br_=br_: V.tensor_tensor(out=w4b[:], in0=ps[br_][:, 0:128], in1=negmask[:, 1, :], op=ALU.mult))
                    tt(w4a[:], "w4a", w4a[:], "w4a", w4b[:], "w4b", ALU.add)
                    dv(["w4a", "dsel", "identf"], [("W4", g)], lambda g=g: V.scalar_tensor_tensor(out=W4[:, g, :], in0=identf[:], scalar=dsel[:, g:g + 1], in1=w4a[:], op0=ALU.mult, op1=ALU.add))
                tr.barrier()
                tr.dma("sp", W1d[l], W1[:], [], [])
                tr.dma("sp", W3d[l], W3[:], [], [])
                tr.dma("sp", W4d[l], W4[:], [], [])
                tr.dma("sp", AP1d[l], AP1[:], [], [])
                tr.dma("sp", AP2d[l], AP2[:], [], [])
                tr.barrier()

        for l_ in layers:
            s5_gen(l_)


        ncount = [0]

        def norm_tile(st_bufs, hsrc, t0, n, scale_ap, shift_ap, out_fn, out_key, hkey_reads):
            hTs, sqs, sd, rstd, tmp = st_bufs
            i = ncount[0] % 2
            ncount[0] += 1
            hT, sq = hTs[i], sqs[i]
            hk, sk = "hTn%d" % i, "sqn%d" % i
            tr.dma("sp", hT[:, :, 0:n], hsrc[:, :, t0:t0 + n], hkey_reads, [hk])
            tr.op("act", [hk], [sk], lambda: nc.scalar.activation(out=sq[:, :, 0:n], in_=hT[:, :, 0:n], func=AF.Square))
            b = bank()

            def mm():
                inst = None
                for k in range(KT):
                    inst = nc.tensor.matmul(ps[b][:, 0:n], onesb[:], sq[:, k, 0:n], start=(k == 0), stop=(k == KT - 1))
                return inst
            tr.op("pe", [sk, "onesb"], [("ps", b)], mm)
            tr.op("act", [("ps", b), "epsc"], ["sd"],
                  lambda: nc.scalar.activation(out=sd[:, 0:n], in_=ps[b][:, 0:n], func=AF.Sqrt, bias=epsc[:, 0:1], scale=1.0 / D))
            tr.op("dve", ["sd"], ["rstd"], lambda: nc.vector.reciprocal(out=rstd[:, 0:n], in_=sd[:, 0:n]))
            tr.op("dve", [hk, "rstd"], ["tmpn"],
                  lambda: nc.vector.tensor_tensor(out=tmp[:, :, 0:n], in0=hT[:, :, 0:n],
                                                  in1=rstd[:, 0:n].unsqueeze(1).to_broadcast([128, KT, n]), op=ALU.mult))

            def ev():
                inst = None
                for k in range(KT):
                    inst = nc.scalar.activation(out=out_fn(k), in_=tmp[:, k, 0:n], func=AF.Identity,
                                                scale=scale_ap(k), bias=shift_ap(k))
                return inst
            tr.op("act", ["tmpn"], [out_key], ev)

        def norm_sb(bufs, h_, hk, n, scale_ap, shift_ap, out_fn, out_key):
            sq, sd, rstd = bufs
            tr.op("act", [hk], ["sqS"], lambda: nc.scalar.activation(out=sq[:, :, 0:n], in_=h_[:, :, 0:n], func=AF.Square))
            b = bank()

            def mm():
                inst = None
                for k in range(KT):
                    inst = nc.tensor.matmul(ps[b][:, 0:n], onesb[:], sq[:, k, 0:n], start=(k == 0), stop=(k == KT - 1))
                return inst
            tr.op("pe", ["sqS", "onesb"], [("ps", b)], mm)
            tr.op("act", [("ps", b), "epsc"], ["sdS"],
                  lambda: nc.scalar.activation(out=sd[:, 0:n], in_=ps[b][:, 0:n], func=AF.Sqrt, bias=epsc[:, 0:1], scale=1.0 / D))
            tr.op("dve", ["sdS"], ["rstdS"], lambda: nc.vector.reciprocal(out=rstd[:, 0:n], in_=sd[:, 0:n]))
            tr.op("dve", [hk, "rstdS"], [hk],
                  lambda: nc.vector.tensor_tensor(out=h_[:, :, 0:n], in0=h_[:, :, 0:n],
                                                  in1=rstd[:, 0:n].unsqueeze(1).to_broadcast([128, KT, n]), op=ALU.mult))

            def ev():
                inst = None
                for k in range(KT):
                    inst = nc.scalar.activation(out=out_fn(k), in_=h_[:, k, 0:n], func=AF.Identity,
                                                scale=scale_ap(k), bias=shift_ap(k))
                return inst
            tr.op("act", [hk], [out_key], ev)

        def alloc_norm_sb(st):
            return (sb("sqS", [128, KT, 512], BF16, st), sb("sdS", [128, 512], F32, st), sb("rstdS", [128, 512], F32, st))

        def alloc_norm_bufs(st):
            return ([sb("hTn%d" % i, [128, KT, 512], F32, st) for i in range(2)], [sb("sqn%d" % i, [128, KT, 512], BF16, st) for i in range(2)],
                    sb("sdn", [128, 512], F32, st), sb("rstdn", [128, 512], F32, st), sb("tmpn", [128, KT, 512], F32, st))

        for s in range(nseq):
            for li, l in enumerate(layers):
                hsrc = xT[s] if li == 0 else hbuf[s]
                hdst = hbuf[s]
                jl = s

                def modcol(isctx):
                    return 2 if isctx else jl

                with contextlib.ExitStack() as mix:
                    uT = sb("uT", [128, 4, LT], BF16, mix)
                    ybT = sb("ybT", [128, 4, LT], BF16, mix)
                    with contextlib.ExitStack() as st:
                        aT = sb("aT", [128, 4, APAD], BF16, st)
                        with contextlib.ExitStack() as sa:
                            scope("A", sa)
                            nb = alloc_norm_bufs(sa) if li == 0 else None
                            wA = sb("wA", [128, 12, KT, 128], BF16, sa)
                            nlt = [sb("nlt%d" % i, [128, KT, 512], BF16, sa) for i in range(3 if li > 0 else 2)]
                            sg = sb("sgA", [128, 4, 512], BF16, sa)
                            for o_ in [0, 1, 2, 3, 8, 4, 9, 5, 10, 6, 11, 7]:
                                tr.dma("pool", wA[:, o_, :, :], w_inA[l, o_], [], [("wA", o_)])
                            for (a, b_) in [(0, 16), (272, 288), (2336, 2352)]:
                                tr.op("pool", [], ["aT"], lambda a=a, b_=b_: nc.gpsimd.memset(aT[:, :, a:b_], 0.0))
                            for ti, (t0, n, isc) in enumerate(TILES):
                                mc = modcol(isc)
                                nl = nlt[ti % len(nlt)]
                                nk = "nlt%d" % (ti % len(nlt))
                                if li == 0:
                                    norm_tile(nb, hsrc, t0, n, lambda k, mc=mc: ms1[l][:, k, mc:mc + 1],
                                              lambda k, mc=mc: mods[l][:, k, mc:mc + 1], lambda k, nl=nl, n=n: nl[:, k, 0:n], nk, [])
                                    tr.dma("sp", nlbuf[:, :, t0:t0 + n], nl[:, :, 0:n], [nk], [("nlbuf", ti)])
                                else:
                                    tr.dma("sp", nl[:, :, 0:n], nlbuf[:, :, t0:t0 + n], [], [nk])
                                p0 = apos(t0)
                                for o in [0, 1, 2, 3, 8, 4, 9, 5, 10, 6, 11, 7]:
                                    b = bank()

                                    def mm(o=o, b=b, nl=nl, n=n):
                                        inst = None
                                        for k in range(KT):
                                            inst = nc.tensor.matmul(ps[b][:, 0:n], wA[:, o, k, :], nl[:, k, 0:n],
                                                                    start=(k == 0), stop=(k == KT - 1))
                                        return inst
                                    tr.op("pe", [("wA", o), nk], [("ps", b)], mm)
                                    if o < 4:
                                        tr.op("act", [("ps", b)], [("uT", ti)],
                                              lambda o=o, b=b, t0=t0, n=n: nc.scalar.activation(
                                                  out=uT[:, o, :].rearrange("p (r q) -> p r q", r=8)[:, :, t0 // 8:(t0 + n) // 8],
                                                  in_=ps[b][:, 0:n].rearrange("p (q r) -> p r q", r=8), func=AF.Copy))
                                    elif o >= 8:
                                        tr.op("act", [("ps", b)], [("sgA", o - 8)],
                                              lambda o=o, b=b, n=n: nc.scalar.activation(out=sg[:, o - 8, 0:n], in_=ps[b][:, 0:n], func=AF.Sigmoid))
                                    else:
                                        tr.op("dve", [("ps", b), ("sgA", o - 4)], ["aT"],
                                              lambda o=o, b=b, n=n, p0=p0: nc.vector.tensor_tensor(
                                                  out=aT[:, o - 4, p0:p0 + n], in0=ps[b][:, 0:n], in1=sg[:, o - 4, 0:n], op=ALU.mult))
                        tr.barrier()
                        if debug and s == 0 and li == 0:
                            tr.dma("sp", dbg_out("uT", [128, 4, LT], BF16), uT[:], [], [])
                            tr.dma("sp", dbg_out("aT", [128, 4, APAD], BF16), aT[:], [], [])
                            tr.barrier()
                        with contextlib.ExitStack() as sc:
                            Z = sb("Z", [128, 2, 32, NQ], BF16, sc)
                            Ucol = sb("Ucol", [128, 32, NQ], BF16, sc)
                            with contextlib.ExitStack() as ssel:
                                scope("C_im2col", ssel)
                                sel = sb("sel", [128, 64, 128], BF16, ssel)
                                for gq_ in range(8):
                                    tr.dma("pool", sel[:, 8 * gq_:8 * gq_ + 8, :], sel1_d[:, 8 * gq_:8 * gq_ + 8, :], [], [("sel", gq_)])
                                V = nc.vector
                                for g in range(32):
                                    c, gq = g // 8, g % 8
                                    b = bank()

                                    def mm(c=c, gq=gq, b=b):
                                        inst = None
                                        ur = uT[:, c, :].rearrange("p (r q) -> p r q", r=8)
                                        for r in range(8):
                                            inst = nc.tensor.matmul(ps[b][:, 0:NQ], sel[:, 8 * gq + r, :], ur[:, r, :], start=(r == 0), stop=(r == 7))
                                        return inst
                                    tr.op("pe", [("sel", gq)] + [("uT", ti) for ti in range(5)], [("ps", b)], mm)
                                    if g % 2 == 0:
                                        tr.op("act", [("ps", b)], [("Ucol", g)], lambda g=g, b=b: nc.scalar.activation(out=Ucol[:, g, :], in_=ps[b][:, 0:NQ], func=AF.Copy))
                                    else:
                                        tr.op("dve", [("ps", b)], [("Ucol", g)], lambda g=g, b=b: V.tensor_copy(out=Ucol[:, g, :], in_=ps[b][:, 0:NQ]))
                            tr.barrier()
                            with contextlib.ExitStack() as sz:
                                W1 = sb("W1", [128, 32, 2, 128], BF16, sz)
                                V = nc.vector
                                tr.dma("sp", W1[:], W1d[l], [], [("W1", g_) for g_ in range(8)])
                                for tau in range(32):
                                    dd, pair = tau // 16, tau % 16
                                    bz = [bank(), bank()]

                                    def mm(tau=tau, pair=pair, bz=bz):
                                        inst = None
                                        for pl in range(2):
                                            for gp in range(2):
                                                inst = nc.tensor.matmul(ps[bz[pl]][64 * gp:64 * gp + 64, 0:NQ], W1[:, tau, pl, 64 * gp:64 * gp + 64],
                                                                        Ucol[:, 2 * pair + gp, :], start=True, stop=True)
                                        return inst
                                    tr.op("pe", [("W1", tau // 4), ("Ucol", 2 * pair), ("Ucol", 2 * pair + 1)], [("ps", bz[0]), ("ps", bz[1])], mm)
                                    for pl in range(2):
                                        en = "act" if pl == 0 else "dve"

                                        def cp(o, i, en=en):
                                            if en == "act":
                                                return nc.scalar.activation(out=o, in_=i, func=AF.Copy)
                                            return V.tensor_copy(out=o, in_=i)
                                        if dd == 0:
                                            tr.op(en, [("ps", bz[pl])], [("Z", tau)], lambda pl=pl, tau=tau, bz=bz, cp=cp: cp(Z[:, pl, tau, :], ps[bz[pl]][:, 0:NQ]))
                                        else:
                                            def ev(pl=pl, tau=tau, bz=bz, cp=cp):
                                                cp(Z[:, pl, tau, 0:QC], ps[bz[pl]][:, QC - 1::-1])
                                                return cp(Z[:, pl, tau, QC:NQ], ps[bz[pl]][:, NQ - 1:QC - 1:-1])
                                            tr.op(en, [("ps", bz[pl])], [("Z", tau)], ev)
                            tr.barrier()
                            with contextlib.ExitStack() as ss:
                                scope("C_scan", ss)
                                AP1 = sb("AP1", [128, 17, 2, 32], F32, ss)
                                AP2 = sb("AP2", [128, 17, 2, 32], F32, ss)
                                tr.dma("sp", AP1[:], AP1d[l], [], ["AP1"])
                                tr.dma("sp", AP2[:], AP2d[l], [], ["AP2"])
                                TS = 22

                                def scan_half(eng, en, ta, tb_):
                                    nt = tb_ - ta
                                    shp = [128, 2, nt, 18]
                                    cur = sb("cur" + en, shp, F32, ss)
                                    T = sb("T32" + en, shp, F32, ss)
                                    M1 = sb("M1" + en, shp, F32, ss)
                                    Xb = sb("Xb" + en, [128, 2, nt, 19], F32, ss)
                                    m1s = sb("m1s" + en, [128, 2, nt], F32, ss)
                                    m2s = sb("m2s" + en, [128, 2, nt], F32, ss)
                                    K_ = lambda nm: nm + en
                                    Tk = K_("T32")

                                    def Zv(j):
                                        return Z[:, :, ta:tb_, :].rearrange("p c t (q j) -> p c t q j", j=16)[:, :, :, :, j]

                                    def Ab(tab, j):
                                        return tab[:, j, :, ta:tb_].unsqueeze(3).to_broadcast([128, 2, nt, 18])

                                    def As(tab, j):
                                        return tab[:, j, :, ta:tb_]
                                    for j in range(16):
                                        if j == 0:
                                            tr.op(en, [], [Tk], lambda: eng.tensor_copy(out=T[:], in_=Zv(0)))
                                        else:
                                            tr.op(en, [K_("cur")], [Tk], lambda j=j: eng.tensor_tensor(out=T[:], in0=cur[:], in1=Zv(j), op=ALU.add))
                                            tr.op("act", [Tk], [K_("Zv%d" % j)], lambda j=j: nc.scalar.activation(out=Zv(j), in_=T[:], func=AF.Copy))
                                        tr.op(en, [Tk, "AP1"], [K_("M1")], lambda: eng.tensor_tensor(out=M1[:], in0=T[:], in1=Ab(AP1, 1), op=ALU.mult))
                                        tr.op(en, [Tk, "AP2"], [K_("cur")], lambda: eng.tensor_tensor(out=cur[:], in0=T[:, ::-1, :, :], in1=Ab(AP2, 1), op=ALU.mult))
                                        tr.op(en, [K_("M1"), K_("cur")], [K_("cur")], lambda: eng.tensor_tensor(out=cur[:], in0=cur[:], in1=M1[:], op=ALU.add))
                                        yield
                                    tr.op(en, [], [K_("Xb")], lambda: eng.memset(Xb[:, :, :, 0], 0.0))
                                    for q in range(18):
                                        tr.op(en, [K_("Xb")], [K_("m1s")], lambda q=q: eng.tensor_tensor(out=m1s[:], in0=Xb[:, :, :, q], in1=As(AP1, 16), op=ALU.mult))
                                        tr.op(en, [K_("Xb")], [K_("m2s")], lambda q=q: eng.tensor_tensor(out=m2s[:], in0=Xb[:, ::-1, :, q], in1=As(AP2, 16), op=ALU.mult))
                                        tr.op(en, [K_("m1s"), K_("m2s")], [K_("m1s")], lambda: eng.tensor_tensor(out=m1s[:], in0=m1s[:], in1=m2s[:], op=ALU.add))
                                        tr.op(en, [K_("m1s"), K_("cur")], [K_("Xb")], lambda q=q: eng.tensor_tensor(out=Xb[:, :, :, q + 1], in0=m1s[:], in1=cur[:, :, :, q], op=ALU.add))
                                        yield
                                    for j in range(16):
                                        zk = K_("Zv%d" % j)
                                        if j == 0:
                                            tr.op(en, [K_("Xb"), zk], [zk], lambda: eng.tensor_tensor(out=Zv(0), in0=Zv(0), in1=Xb[:, :, :, 0:18], op=ALU.add))
                                            yield
                                            continue
                                        tr.op(en, [K_("Xb")], [K_("M1")], lambda j=j: eng.tensor_tensor(out=M1[:], in0=Xb[:, :, :, 0:18], in1=Ab(AP1, j), op=ALU.mult))
                                        tr.op(en, [K_("Xb")], [K_("cur")], lambda j=j: eng.tensor_tensor(out=cur[:], in0=Xb[:, ::-1, :, 0:18], in1=Ab(AP2, j), op=ALU.mult))
                                        tr.op(en, [K_("M1"), K_("cur")], [K_("M1")], lambda: eng.tensor_tensor(out=M1[:], in0=M1[:], in1=cur[:], op=ALU.add))
                                        tr.op(en, [K_("M1"), zk], [zk], lambda j=j: eng.tensor_tensor(out=Zv(j), in0=Zv(j), in1=M1[:], op=ALU.add))
                                        yield

                                def phaseB():
                                    dg = sb("dg31", [128, 4, 31, 128], BF16, ss)
                                    xsq = sb("xsq", [128, 4, 512], BF16, ss)
                                    uflat = uT[:].rearrange("p c t -> p (c t)")
                                    uf32 = uflat.bitcast(F32)
                                    x32 = uf32[:, 0:2048].rearrange("p (c n) -> p c n", c=4)
                                    mu = uf32[:, 2048:2560]
                                    var = uf32[:, 2560:3072]
                                    rs = uf32[:, 3072:3584]
                                    xb = uflat[:, 7168:9216].rearrange("p (c n) -> p c n", c=4)
                                    for c in range(4):
                                        def mk(c=c):
                                            inst = None
                                            for k in range(31):
                                                inst = nc.vector.tensor_scalar(out=dg[:, c, k, :], in0=identf[:], scalar1=cdw_s[:, l, c, k:k + 1],
                                                                               scalar2=None, op0=ALU.mult)
                                            return inst
                                        tr.op("dve", [], [("dg", c)], mk)
                                        yield
                                    for ti, (t0, n, isc) in enumerate(TILES):
                                        p0 = apos(t0)
                                        for c in range(4):
                                            b = bank()

                                            def mm(c=c, b=b, n=n, p0=p0):
                                                inst = None
                                                for k in range(31):
                                                    inst = nc.tensor.matmul(ps[b][:, 0:n], dg[:, c, k, :], aT[:, c, p0 + k - 15:p0 + k - 15 + n],
                                                                            start=(k == 0), stop=(k == 30))
                                                return inst
                                            tr.op("pe", [("dg", c), "aT"], [("ps", b)], mm)
                                            tr.op("act", [("ps", b)], [("x32", c)],
                                                  lambda c=c, b=b, n=n: nc.scalar.activation(out=x32[:, c, 0:n], in_=ps[b][:, 0:n], func=AF.Identity,
                                                                                             bias=cdb_s[:, l, c:c + 1], scale=1.0))
                                            tr.op("act", [("ps", b)], [("xsq", c)],
                                                  lambda c=c, b=b, n=n: nc.scalar.activation(out=xsq[:, c, 0:n], in_=ps[b][:, 0:n], func=AF.Square,
                                                                                             bias=cdb_s[:, l, c:c + 1], scale=1.0))
                                            tr.op("dve", [("x32", c)], [("xb", c)],
                                                  lambda c=c, n=n: nc.vector.tensor_copy(out=xb[:, c, 0:n], in_=x32[:, c, 0:n]))
                                            yield
                                        b1 = bank()
                                        b2 = bank()

                                        def mm2(b1=b1, b2=b2, n=n):
                                            inst = None
                                            for c in range(4):
                                                nc.tensor.matmul(ps[b1][:, 0:n], o512b[:], xb[:, c, 0:n], start=(c == 0), stop=(c == 3))
                                            for c in range(4):
                                                inst = nc.tensor.matmul(ps[b2][:, 0:n], o512b[:], xsq[:, c, 0:n], start=(c == 0), stop=(c == 3))
                                            return inst
                                        tr.op("pe", [("xb", c) for c in range(4)] + [("xsq", c) for c in range(4)] + ["o512b"], [("ps", b1), ("ps", b2)], mm2)
                                        tr.op("act", [("ps", b1)], ["mu"], lambda b1=b1, n=n: nc.scalar.activation(out=mu[:, 0:n], in_=ps[b1][:, 0:n], func=AF.Copy))
                                        tr.op("dve", ["mu"], ["var"], lambda n=n: nc.vector.tensor_tensor(out=var[:, 0:n], in0=mu[:, 0:n], in1=mu[:, 0:n], op=ALU.mult))
                                        tr.op("dve", ["var", ("ps", b2)], ["var"],
                                              lambda b2=b2, n=n: nc.vector.tensor_tensor(out=var[:, 0:n], in0=ps[b2][:, 0:n], in1=var[:, 0:n], op=ALU.subtract))
                                        tr.op("dve", ["var"], ["var"], lambda n=n: nc.vector.tensor_scalar(out=var[:, 0:n], in0=var[:, 0:n], scalar1=0.0, scalar2=None, op0=ALU.max))
                                        tr.op("act", ["var", "epsc"], ["rs"],
                                              lambda n=n: nc.scalar.activation(out=rs[:, 0:n], in_=var[:, 0:n], func=AF.Sqrt, bias=epsc[:, 0:1], scale=1.0))
                                        tr.op("dve", ["rs"], ["rs"], lambda n=n: nc.vector.reciprocal(out=rs[:, 0:n], in_=rs[:, 0:n]))
                                        allx = [("x32", c) for c in range(4)]
                                        tr.op("dve", allx + ["mu"], allx,
                                              lambda n=n: nc.vector.tensor_tensor(out=x32[:, :, 0:n], in0=x32[:, :, 0:n],
                                                                                  in1=mu[:, 0:n].unsqueeze(1).to_broadcast([128, 4, n]), op=ALU.subtract))
                                        tr.op("dve", allx + ["rs"], allx,
                                              lambda n=n: nc.vector.tensor_tensor(out=x32[:, :, 0:n], in0=x32[:, :, 0:n],
                                                                                  in1=rs[:, 0:n].unsqueeze(1).to_broadcast([128, 4, n]), op=ALU.mult))

                                        def ev(t0=t0, n=n):
                                            inst = None
                                            for c in range(4):
                                                inst = nc.scalar.activation(out=ybT[:, c, t0:t0 + n], in_=x32[:, c, 0:n], func=AF.Silu,
                                                                            scale=clg_s[:, l, c:c + 1], bias=clb_s[:, l, c:c + 1])
                                            return inst
                                        tr.op("act", allx, [("ybT", ti)], ev)
                                        yield

                                gens = [[scan_half(nc.vector, "dve", 0, TS), 2], [scan_half(nc.gpsimd, "pool", TS, 32), 2], [phaseB(), 1]]
                                while gens:
                                    for ent in list(gens):
                                        for _ in range(ent[1]):
                                            try:
                                                next(ent[0])
                                            except StopIteration:
                                                gens.remove(ent)
                                                break
                            tr.barrier()
                            with contextlib.ExitStack() as sy:
                                W3 = sb("W3", [128, 2, 32, 128], BF16, sy)
                                W4 = sb("W4", [128, 32, 128], BF16, sy)
                                tr.dma("sp", W3[:], W3d[l], [], ["W3"])
                                tr.dma("sp", W4[:], W4d[l], [], ["W4"])

                                for g in range(32):
                                    gp, pair = g % 2, g // 2
                                    R = slice(64 * gp, 64 * gp + 64)
                                    b = bank()

                                    def mm(g=g, R=R, pair=pair, b=b):
                                        o = ps[b]
                                        nc.tensor.matmul(o[:, 0:NQ], W4[:, g, :], Ucol[:, g, :], start=True, stop=False)
                                        for pl in range(2):
                                            nc.tensor.matmul(o[:, 0:NQ], W3[R, pl, pair, :], Z[R, pl, pair, :], start=False, stop=False, skip_group_check=True)
                                        inst = None
                                        for pl in range(2):
                                            tau = 16 + pair
                                            nc.tensor.matmul(o[:, 0:QC], W3[R, pl, tau, :], Z[R, pl, tau, QC - 1::-1], start=False, stop=False, skip_group_check=True)
                                            inst = nc.tensor.matmul(o[:, QC:NQ], W3[R, pl, tau, :], Z[R, pl, tau, NQ - 1:QC - 1:-1], start=False, stop=(pl == 1), skip_group_check=True)
                                        return inst
                                    tr.op("pe", ["W3", "W4"], [("ps", b)], mm)
                                    if g % 2 == 0:
                                        tr.op("act", [("ps", b)], [("Ycol", g)], lambda g=g, b=b: nc.scalar.activation(
                                            out=uT[:, g // 8, (g % 8) * NQ:(g % 8 + 1) * NQ], in_=ps[b][:, 0:NQ], func=AF.Copy))
                                    else:
                                        tr.op("dve", [("ps", b)], [("Ycol", g)], lambda g=g, b=b: nc.vector.tensor_copy(
                                            out=uT[:, g // 8, (g % 8) * NQ:(g % 8 + 1) * NQ], in_=ps[b][:, 0:NQ]))
                            tr.barrier()
                            if debug and s == 0 and li == 0:
                                tr.dma("sp", dbg_out("Ycol", [128, 4, LT], BF16), uT[:], [], [])
                                tr.barrier()
                            gT = Ucol[:].rearrange("p g q -> p (g q)").rearrange("p (c t) -> p c t", c=4)
                            with contextlib.ExitStack() as sg2:
                                scope("C_un", sg2)
                                sel = sb("sel", [128, 64, 128], BF16, sg2)
                                for r_ in range(8):
                                    tr.dma("pool", sel[:, 8 * r_:8 * r_ + 8, :], sel2_d[:, 8 * r_:8 * r_ + 8, :], [], [("sel2", r_)])
                                xg = [sb("xg%d" % i, [128, NQ], F32, sg2) for i in range(4)]
                                tg = [sb("tg%d" % i, [128, NQ], F32, sg2) for i in range(4)]
                                pi_ = 0
                                for c in range(4):
                                    for r in range(8):
                                        b = bank()
                                        x_, t_ = xg[pi_ % 4], tg[pi_ % 4]
                                        xk, tk = "xg%d" % (pi_ % 4), "tg%d" % (pi_ % 4)
                                        pi_ += 1

                                        def mm(c=c, r=r, b=b):
                                            inst = None
                                            for gq in range(8):
                                                inst = nc.tensor.matmul(ps[b][:, 0:NQ], sel[:, 8 * r + gq, :], uT[:, c, gq * NQ:(gq + 1) * NQ],
                                                                        start=(gq == 0), stop=(gq == 7))
                                            return inst
                                        tr.op("pe", [("sel2", r)], [("ps", b)], mm)
                                        tr.op("act", [("ps", b)], [xk], lambda x_=x_, b=b: nc.scalar.activation(out=x_[:], in_=ps[b][:, 0:NQ], func=AF.Copy))
                                        tr.op("pool", [xk], [tk], lambda x_=x_, t_=t_: nc.gpsimd.tensor_tensor(out=t_[:], in0=x_[:], in1=x_[:], op=ALU.mult))
                                        tr.op("pool", [tk], [tk], lambda t_=t_: nc.gpsimd.tensor_scalar(out=t_[:], in0=t_[:], scalar1=0.044715, scalar2=1.0, op0=ALU.mult, op1=ALU.add))
                                        tr.op("dve", [tk, xk], [tk], lambda x_=x_, t_=t_: nc.vector.tensor_tensor(out=t_[:], in0=t_[:], in1=x_[:], op=ALU.mult))
                                        tr.op("act", [tk], [tk], lambda t_=t_: nc.scalar.activation(out=t_[:], in_=t_[:], func=AF.Sigmoid, scale=2.0 * math.sqrt(2.0 / math.pi)))
                                        tr.op("dve", [tk, xk], ["gT"], lambda x_=x_, t_=t_, c=c, r=r: nc.vector.tensor_tensor(
                                            out=gT[:, c, :].rearrange("p (q r) -> p r q", r=8)[:, r, :], in0=x_[:], in1=t_[:], op=ALU.mult))
                            tr.barrier()
                            with contextlib.ExitStack() as sg3:
                                scope("C_glu", sg3)
                                wg = sb("wglu", [128, 4, 512], BF16, sg3)
                                sgl = [sb("sgl%d" % i, [128, 512], BF16, sg3) for i in range(2)]
                                tr.dma("pool", wg[:], w_glu[l], [], ["wglu"])
                                pi_ = 0
                                for ti, (t0, n, isc) in enumerate(TILES):
                                    for o in range(4):
                                        b = bank()
                                        sl, sk = sgl[pi_ % 2], "sgl%d" % (pi_ % 2)
                                        pi_ += 1

                                        def mm(o=o, b=b, t0=t0, n=n):
                                            inst = None
                                            for k in range(4):
                                                inst = nc.tensor.matmul(ps[b][:, 0:n], wg[:, k, 128 * o:128 * (o + 1)], gT[:, k, t0:t0 + n], start=(k == 0), stop=(k == 3))
                                            return inst
                                        tr.op("pe", ["wglu"], [("ps", b)], mm)
                                        tr.op("act", [("ps", b)], [sk], lambda sl=sl, b=b, n=n: nc.scalar.activation(out=sl[:, 0:n], in_=ps[b][:, 0:n], func=AF.Sigmoid))
                                        tr.op("dve", [sk], [("yaT", ti)], lambda sl=sl, o=o, t0=t0, n=n: nc.vector.tensor_tensor(
                                            out=uT[:, o, t0:t0 + n], in0=gT[:, o, t0:t0 + n], in1=sl[:, 0:n], op=ALU.mult))
                            tr.barrier()
                        if debug and s == 0 and li == 0:
                            tr.dma("sp", dbg_out("yaT", [128, 4, LT], BF16), uT[:], [], [])
                            tr.barrier()

                    with contextlib.ExitStack() as sd_:
                        scope("D", sd_)
                        wa = sb("wa", [128, 8, 4, 128], BF16, sd_)
                        wb_ = sb("wb", [128, 8, 4, 128], BF16, sd_)
                        wgt = sb("wgt", [128, 16, KT, 128], BF16, sd_)
                        wo = sb("wo", [128, 8, KT, 128], BF16, sd_)
                        hT = [sb("hTd%d" % i, [128, KT, 512], F32, sd_) for i in range(2)]
                        nlt = [sb("nld%d" % i, [128, KT, 512], BF16, sd_) for i in range(2)]
                        mT = sb("mT", [128, KT, 512], BF16, sd_)
                        s3 = [sb("s3_%d" % i, [128, 512], F32, sd_) for i in range(2)]
                        s4 = [sb("s4_%d" % i, [128, 512], F32, sd_) for i in range(2)]
                        nsb = alloc_norm_sb(sd_)
                        nlo = sb("nlo", [128, KT, 512], BF16, sd_)
                        for o_ in range(8):
                            tr.dma("pool", wa[:, o_, :, :], w_a[l, o_], [], [("wa", o_)])
                            tr.dma("pool", wb_[:, o_, :, :], w_b[l, o_], [], [("wb", o_)])
                            tr.dma("pool", wgt[:, o_, :, :], w_inG[l, o_], [], [("wgt", o_)])
                            tr.dma("pool", wgt[:, 8 + o_, :, :], w_inG[l, 8 + o_], [], [("wgt", 8 + o_)])
                        for o_ in range(8):
                            tr.dma("pool", wo[:, o_, :, :], w_out[l, o_], [], [("wo", o_)])
                        pi_ = 0
                        for ti, (t0, n, isc) in enumerate(TILES):
                            mc = modcol(isc)
                            h_, hk = hT[ti % 2], "hTd%d" % (ti % 2)
                            nl, nk = nlt[ti % 2], "nld%d" % (ti % 2)

                            def loadD(tj):
                                (ta_, na_, _) = TILES[tj]
                                tr.dma("sp", hT[tj % 2][:, :, 0:na_], hsrc[:, :, ta_:ta_ + na_], [], ["hTd%d" % (tj % 2)])
                                tr.dma("sp", nlt[tj % 2][:, :, 0:na_], nlbuf[:, :, ta_:ta_ + na_], [], ["nld%d" % (tj % 2)])
                            if ti == 0:
                                loadD(0)
                            if ti + 1 < len(TILES):
                                loadD(ti + 1)
                            for o in range(KT):
                                b1, b2, b3, b4 = bank(), bank(), bank(), bank()
                                a3, a4 = s3[pi_ % 2], s4[pi_ % 2]
                                k3, k4 = "s3_%d" % (pi_ % 2), "s4_%d" % (pi_ % 2)
                                pi_ += 1

                                def mm(o=o, b1=b1, b2=b2, b3=b3, b4=b4, t0=t0, n=n, nl=nl):
                                    inst = None
                                    for k in range(4):
                                        nc.tensor.matmul(ps[b1][:, 0:n], wa[:, o, k, :], uT[:, k, t0:t0 + n], start=(k == 0), stop=(k == 3))
                                    for k in range(4):
                                        nc.tensor.matmul(ps[b2][:, 0:n], wb_[:, o, k, :], ybT[:, k, t0:t0 + n], start=(k == 0), stop=(k == 3))
                                    for k in range(KT):
                                        nc.tensor.matmul(ps[b3][:, 0:n], wgt[:, o, k, :], nl[:, k, 0:n], start=(k == 0), stop=(k == KT - 1))
                                    for k in range(KT):
                                        inst = nc.tensor.matmul(ps[b4][:, 0:n], wgt[:, 8 + o, k, :], nl[:, k, 0:n], start=(k == 0), stop=(k == KT - 1))
                                    return inst
                                tr.op("pe", [("wa", o), ("wb", o), ("wgt", o), ("wgt", 8 + o), nk], [("ps", b1), ("ps", b2), ("ps", b3), ("ps", b4)], mm)
                                tr.op("act", [("ps", b3)], [k3], lambda a3=a3, b3=b3, n=n: nc.scalar.activation(out=a3[:, 0:n], in_=ps[b3][:, 0:n], func=AF.Sigmoid))
                                tr.op("act", [("ps", b4)], [k4], lambda a4=a4, b4=b4, n=n: nc.scalar.activation(out=a4[:, 0:n], in_=ps[b4][:, 0:n], func=AF.Sigmoid))
                                tr.op("dve", [("ps", b1), k3], [k3], lambda a3=a3, b1=b1, n=n: nc.vector.tensor_tensor(out=a3[:, 0:n], in0=ps[b1][:, 0:n], in1=a3[:, 0:n], op=ALU.mult))
                                tr.op("dve", [("ps", b2), k4], [k4], lambda a4=a4, b2=b2, n=n: nc.vector.tensor_tensor(out=a4[:, 0:n], in0=ps[b2][:, 0:n], in1=a4[:, 0:n], op=ALU.mult))
                                tr.op("dve", [k3, k4], [("mT", o)], lambda a3=a3, a4=a4, o=o, n=n: nc.vector.tensor_tensor(out=mT[:, o, 0:n], in0=a3[:, 0:n], in1=a4[:, 0:n], op=ALU.add))
                            for o2 in range(KT):
                                b = bank()

                                def mm(o2=o2, b=b, n=n):
                                    inst = None
                                    for k in range(KT):
                                        inst = nc.tensor.matmul(ps[b][:, 0:n], wo[:, o2, k, :], mT[:, k, 0:n], start=(k == 0), stop=(k == KT - 1))
                                    return inst
                                tr.op("pe", [("wo", o2)] + [("mT", o) for o in range(KT)], [("ps", b)], mm)
                                tr.op("dve", [("ps", b), hk], [hk], lambda o2=o2, b=b, n=n, h_=h_, mc=mc: nc.vector.scalar_tensor_tensor(
                                    out=h_[:, o2, 0:n], in0=ps[b][:, 0:n], scalar=mods[l][:, 16 + o2, mc:mc + 1], in1=h_[:, o2, 0:n], op0=ALU.mult, op1=ALU.add))
                            tr.dma("sp", hdst[:, :, t0:t0 + n], h_[:, :, 0:n], [hk], [("hbuf", ti)])
                            norm_sb(nsb, h_, hk, n, lambda k, mc=mc: ms2[l][:, k, mc:mc + 1], lambda k, mc=mc: mods[l][:, 24 + k, mc:mc + 1],
                                    lambda k, n=n: nlo[:, k, 0:n], "nlo")
                            tr.dma("sp", nlbuf2[:, :, t0:t0 + n], nlo[:, :, 0:n], ["nlo"], [("nlbuf2", ti)])
                    tr.barrier()
                if debug and s == 0 and li == 0:
                    tr.dma("sp", dbg_out("hmix", [128, KT, LT], F32), hbuf[s], [], [])
                    tr.barrier()

                with contextlib.ExitStack() as sf:
                    nl2 = sb("nl2", [128, KT, LT], BF16, sf)
                    for ti, (t0, n, isc) in enumerate(TILES):
                        tr.dma("sp", nl2[:, :, t0:t0 + n], nlbuf2[:, :, t0:t0 + n], [], [("nl2", ti)])
                    for half in range(2):
                        with contextlib.ExitStack() as sh:
                            act = sb("actF", [128, 11, LT], BF16, sh)
                            wd = sb("wd", [128, 11, D], BF16, sh)
                            tr.dma("pool", wd[:], w_dn[l, :, 11 * half:11 * half + 11, :], [], ["wd"])
                            with contextlib.ExitStack() as su:
                                scope("E_up%d" % half, su)
                                wu = [sb("wu%d" % i, [128, KT, 256], BF16, su) for i in range(3)]
                                dg9 = [sb("dg9_%d" % i, [128, 9, 128], BF16, su) for i in range(2)]
                                Gc = [sb("Gc%d" % i, [128, GCTX], BF16, su) for i in range(2)]
                                Gl = [sb("Gl%d" % i, [128, GLAT], BF16, su) for i in range(2)]
                                vT = [sb("vT%d" % i, [128, LT], BF16, su) for i in range(2)]
                                tsl = [sb("tsl%d" % i, [128, 512], BF16, su) for i in range(2)]
                                for i in range(2):
                                    tr.op("pool", [], [("G", i)], lambda i=i: nc.gpsimd.memset(Gc[i][:], 0.0))
                                    tr.op("pool", [], [("G", i)], lambda i=i: nc.gpsimd.memset(Gl[i][:], 0.0))
                                pi_ = 0
                                for jj in range(11):
                                    j = 11 * half + jj
                                    w_, wk = wu[jj % 3], "wu%d" % (jj % 3)
                                    d9, dk = dg9[jj % 2], "dg9_%d" % (jj % 2)
                                    gc, gl, gk = Gc[jj % 2], Gl[jj % 2], ("G", jj % 2)
                                    v_, vk = vT[jj % 2], "vT%d" % (jj % 2)
                                    gl3 = gl[:].rearrange("p (a b) -> p a b", b=66)
                                    tr.dma("pool", w_[:], w_up[l, j], [], [wk])

                                    def mk(d9=d9, j=j):
                                        inst = None
                                        for t in range(9):
                                            inst = nc.vector.tensor_scalar(out=d9[:, t, :], in0=identf[:], scalar1=fdw_s[:, l, j, t:t + 1], scalar2=None, op0=ALU.mult)
                                        return inst
                                    tr.op("dve", [], [dk], mk)
                                    for ti, (t0, n, isc) in enumerate(TILES):
                                        bg, bv = bank(), bank()

                                        def mm(bg=bg, bv=bv, t0=t0, n=n, w_=w_):
                                            inst = None
                                            for k in range(KT):
                                                nc.tensor.matmul(ps[bg][:, 0:n], w_[:, k, 0:128], nl2[:, k, t0:t0 + n], start=(k == 0), stop=(k == KT - 1))
                                            for k in range(KT):
                                                inst = nc.tensor.matmul(ps[bv][:, 0:n], w_[:, k, 128:256], nl2[:, k, t0:t0 + n], start=(k == 0), stop=(k == KT - 1))
                                            return inst
                                        tr.op("pe", [wk, ("nl2", ti)], [("ps", bg), ("ps", bv)], mm)
                                        if isc:
                                            tr.op("act", [("ps", bg)], [gk], lambda gc=gc, bg=bg: nc.scalar.activation(out=gc[:, 1:257], in_=ps[bg][:, 0:256], func=AF.Copy))
                                        else:
                                            r0 = 1 + 8 * (ti - 1)
                                            tr.op("act", [("ps", bg)], [gk], lambda gl3=gl3, bg=bg, r0=r0: nc.scalar.activation(
                                                out=gl3[:, r0:r0 + 8, 1:65], in_=ps[bg][:, 0:512].rearrange("p (a b) -> p a b", b=64), func=AF.Copy))
                                        tr.op("dve", [("ps", bv)], [vk], lambda v_=v_, bv=bv, t0=t0, n=n: nc.vector.tensor_copy(out=v_[:, t0:t0 + n], in_=ps[bv][:, 0:n]))
                                    for ti, (t0, n, isc) in enumerate(TILES):
                                        b = bank()
                                        ts_, tk = tsl[pi_ % 2], "tsl%d" % (pi_ % 2)
                                        pi_ += 1
                                        if isc:
                                            def mm(b=b, d9=d9, gc=gc):
                                                inst = None
                                                for dx in (-1, 0, 1):
                                                    inst = nc.tensor.matmul(ps[b][:, 0:256], d9[:, 4 + dx, :], gc[:, 1 + dx:257 + dx], start=(dx == -1), stop=(dx == 1))
                                                return inst
                                        else:
                                            r0 = 1 + 8 * (ti - 1)

                                            def mm(b=b, d9=d9, gl3=gl3, r0=r0):
                                                inst = None
                                                o3 = ps[b][:, 0:512].rearrange("p (a b) -> p a b", b=64)
                                                for t in range(9):
                                                    dy, dx = t // 3 - 1, t % 3 - 1
                                                    inst = nc.tensor.matmul(o3, d9[:, t, :], gl3[:, r0 + dy:r0 + dy + 8, 1 + dx:65 + dx], start=(t == 0), stop=(t == 8))
                                                return inst
                                        tr.op("pe", [dk, gk], [("ps", b)], mm)
                                        tr.op("act", [("ps", b)], [tk], lambda ts_=ts_, b=b, n=n, j=j: nc.scalar.activation(
                                            out=ts_[:, 0:n], in_=ps[b][:, 0:n], func=AF.Silu, bias=fdb_s[:, l, j:j + 1], scale=1.0))
                                        tr.op("dve", [tk, vk], [("act", jj)], lambda ts_=ts_, v_=v_, jj=jj, t0=t0, n=n: nc.vector.tensor_tensor(
                                            out=act[:, jj, t0:t0 + n], in0=ts_[:, 0:n], in1=v_[:, t0:t0 + n], op=ALU.mult))
                            tr.barrier()
                            with contextlib.ExitStack() as sdn:
                                scope("E_dn%d" % half, sdn)
                                hT = [sb("hTe%d" % i, [128, KT, 512], F32, sdn) for i in range(2)]
                                do_final = final and l == last_layer and half == 1
                                do_next = half == 1 and li + 1 < len(layers)
                                if do_next:
                                    lnext = layers[li + 1]
                                    nsb = alloc_norm_sb(sdn)
                                    nlo = sb("nlo", [128, KT, 512], BF16, sdn)
                                if do_final:
                                    sqf = sb("sqf", [128, KT, 512], BF16, sdn)
                                    sdf = sb("sdf", [128, 512], F32, sdn)
                                    rsf = sb("rsf", [128, 512], F32, sdn)
                                    of = sb("of", [128, KT, 512], F32, sdn)
                                for ti, (t0, n, isc) in enumerate(TILES):
                                    mc = modcol(isc)
                                    h_, hk = hT[ti % 2], "hTe%d" % (ti % 2)

                                    def loadE(tj):
                                        (ta_, na_, _) = TILES[tj]
                                        tr.dma("sp", hT[tj % 2][:, :, 0:na_], hdst[:, :, ta_:ta_ + na_], [], ["hTe%d" % (tj % 2)])
                                    if ti == 0:
                                        loadE(0)
                                    if ti + 1 < len(TILES):
                                        loadE(ti + 1)
                                    for o in range(KT):
                                        b = bank()

                                        def mm(o=o, b=b, t0=t0, n=n):
                                            inst = None
                                            for jj in range(11):
                                                inst = nc.tensor.matmul(ps[b][:, 0:n], wd[:, jj, 128 * o:128 * (o + 1)], act[:, jj, t0:t0 + n], start=(jj == 0), stop=(jj == 10))
                                            return inst
                                        tr.op("pe", ["wd"], [("ps", b)], mm)
                                        tr.op("dve", [("ps", b), hk], [hk], lambda o=o, b=b, n=n, h_=h_, mc=mc: nc.vector.scalar_tensor_tensor(
                                            out=h_[:, o, 0:n], in0=ps[b][:, 0:n], scalar=mods[l][:, 40 + o, mc:mc + 1], in1=h_[:, o, 0:n], op0=ALU.mult, op1=ALU.add))
                                    if do_final:
                                        if isc:
                                            continue
                                        tr.op("act", [hk], ["sqf"], lambda h_=h_: nc.scalar.activation(out=sqf[:], in_=h_[:], func=AF.Square))
                                        b = bank()

                                        def mm(b=b):
                                            inst = None
                                            for k in range(KT):
                                                inst = nc.tensor.matmul(ps[b][:, 0:512], onesb[:], sqf[:, k, :], start=(k == 0), stop=(k == KT - 1))
                                            return inst
                                        tr.op("pe", ["sqf"], [("ps", b)], mm)
                                        tr.op("act", [("ps", b)], ["sdf"], lambda b=b: nc.scalar.activation(out=sdf[:], in_=ps[b][:, 0:512], func=AF.Sqrt, bias=epsc[:, 0:1], scale=1.0 / D))
                                        tr.op("dve", ["sdf"], ["rsf"], lambda: nc.vector.reciprocal(out=rsf[:], in_=sdf[:]))
                                        tr.op("dve", [hk, "rsf"], ["of"], lambda h_=h_: nc.vector.tensor_tensor(
                                            out=of[:], in0=h_[:], in1=rsf[:].unsqueeze(1).to_broadcast([128, KT, 512]), op=ALU.mult))
                                        tr.op("dve", ["of"], ["of"], lambda: nc.vector.tensor_tensor(
                                            out=of[:], in0=of[:], in1=fing_s[:].unsqueeze(2).to_broadcast([128, KT, 512]), op=ALU.mult))
                                        tr.dma("sp", out_d[s, :, :, t0 - LCTX:t0 - LCTX + n], of[:], ["of"], [])
                                        if not fused:
                                            tr.dma("sp", hdst[:, :, t0:t0 + n], h_[:, :, 0:n], [hk], [("hbuf", ti)])
                                    else:
                                        tr.dma("sp", hdst[:, :, t0:t0 + n], h_[:, :, 0:n], [hk], [("hbuf", ti)])
                                        if do_next:
                                            norm_sb(nsb, h_, hk, n, lambda k, mc=mc: ms1[lnext][:, k, mc:mc + 1], lambda k, mc=mc: mods[lnext][:, k, mc:mc + 1],
                                                    lambda k, n=n: nlo[:, k, 0:n], "nlo")
                                            tr.dma("sp", nlbuf[:, :, t0:t0 + n], nlo[:, :, 0:n], ["nlo"], [("nlbuf", ti)])
                            tr.barrier()
        tr.finish("sp")
    return nc, dbg


def _kt(w):
    K, O = w.shape
    return np.ascontiguousarray(w.reshape(K // 128, 128, O).transpose(1, 0, 2))


def _vec(v, n):
    return np.ascontiguousarray(v.reshape(n, 128).T)


def prep_weights(inp):
    f = lambda a: np.asarray(a, dtype=np.float32)
    W = {}
    W["ada_w"] = np.stack([_kt(f(inp["ada_w"][l])) for l in range(DEPTH)])
    W["ada_b"] = np.stack([_vec(f(inp["ada_b"][l]), 48) for l in range(DEPTH)])
    W["n1g"] = np.ascontiguousarray(np.stack([_vec(f(inp["norm1_g"][l]), KT) for l in range(DEPTH)], axis=1))
    W["n2g"] = np.ascontiguousarray(np.stack([_vec(f(inp["norm2_g"][l]), KT) for l in range(DEPTH)], axis=1))
    W["fing"] = _vec(f(inp["final_g"]), KT)
    win = [_kt(f(inp["w_in"][l])) for l in range(DEPTH)]
    def om(w):
        p, k, o = w.shape
        return np.ascontiguousarray(w.reshape(p, k, o // 128, 128).transpose(2, 0, 1, 3))
    W["w_inA"] = np.stack([om(w[:, :, :1536]) for w in win])
    W["w_inG"] = np.stack([om(w[:, :, 1536:]) for w in win])
    W["w_glu"] = np.stack([_kt(f(inp["w_glu"][l])) for l in range(DEPTH)])
    W["w_a"] = np.stack([om(_kt(f(inp["w_a"][l]))) for l in range(DEPTH)])
    W["w_b"] = np.stack([om(_kt(f(inp["w_b"][l]))) for l in range(DEPTH)])
    W["w_out"] = np.stack([om(_kt(f(inp["w_out"][l]))) for l in range(DEPTH)])
    up = np.stack([_kt(f(inp["ffn_w_up"][l])) for l in range(DEPTH)])
    g = up[:, :, :, :2816].reshape(DEPTH, 128, KT, NH, 128)
    v = up[:, :, :, 2816:].reshape(DEPTH, 128, KT, NH, 128)
    W["w_up"] = np.ascontiguousarray(np.concatenate([g, v], axis=-1).transpose(0, 3, 1, 2, 4))
    W["w_dn"] = np.stack([_kt(f(inp["ffn_w_down"][l])) for l in range(DEPTH)])
    W["cdw"] = np.ascontiguousarray(f(inp["conv_dw"]).reshape(DEPTH, 31, 4, 128).transpose(3, 0, 2, 1))
    for nm, key in [("cdb", "conv_dw_b"), ("clg", "conv_ln_g"), ("clb", "conv_ln_b")]:
        W[nm] = np.ascontiguousarray(f(inp[key]).reshape(DEPTH, 4, 128).transpose(2, 0, 1))
    W["fdw"] = np.ascontiguousarray(f(inp["ffn_dw"]).reshape(DEPTH, 9, NH, 128).transpose(3, 0, 2, 1))
    W["fdb"] = np.ascontiguousarray(f(inp["ffn_dw_b"]).reshape(DEPTH, NH, 128).transpose(2, 0, 1))

    def s5lay(a):
        L = a.shape[0]
        rest = a.shape[4:]
        a = a.reshape(L, 2, 16, 2, 64, *rest)
        a = np.moveaxis(a, [3, 4, 1, 2], [1, 2, 3, 4])
        return np.ascontiguousarray(a.reshape(L, 128, 32, *rest))
    W["lamre"] = s5lay(f(inp["s5_lam_re"]))
    W["lamim"] = s5lay(f(inp["s5_lam_im"]))
    W["logdt"] = s5lay(np.broadcast_to(f(inp["s5_log_dt"])[:, :, :, None], (DEPTH, 2, 32, 64)).copy())
    W["bre"] = s5lay(f(inp["s5_b_re"]))
    W["bim"] = s5lay(f(inp["s5_b_im"]))
    W["cre"] = s5lay(np.swapaxes(f(inp["s5_c_re"]), 3, 4))
    W["cim"] = s5lay(np.swapaxes(f(inp["s5_c_im"]), 3, 4))
    d = f(inp["s5_d"]).reshape(DEPTH, 32, 16)
    W["s5d"] = np.ascontiguousarray(np.broadcast_to(d.transpose(0, 2, 1)[:, None, :, :], (DEPTH, 8, 16, 32)).reshape(DEPTH, 128, 32))
    W["ident"] = np.eye(128, dtype=np.float32)
    sel1 = np.zeros((128, 64, 128), np.float32)
    sel2 = np.zeros((128, 64, 128), np.float32)
    for gq in range(8):
        for r in range(8):
            for h in range(16):
                sel1[16 * gq + h, 8 * gq + r, 16 * r + h] = 1.0
                sel2[16 * r + h, 8 * r + gq, 16 * gq + h] = 1.0
    W["sel1"] = sel1
    W["sel2"] = sel2
    ev = np.zeros((128, 8, 32), np.float32)
    for r in range(8):
        ev[:, r, :16] = r + 1
        ev[:, r, 16:] = 8 - r
    W["evals"] = ev
    nm = np.zeros((128, 2, 128), np.float32)
    rin = np.arange(128)[:, None] // 16
    rout = np.arange(128)[None, :] // 16
    nm[:, 0, :] = -(rin > rout).astype(np.float32)
    nm[:, 1, :] = -(rin < rout).astype(np.float32)
    W["negmask"] = nm
    W["jvals"] = np.ascontiguousarray(np.broadcast_to((8.0 * np.arange(1, 17, dtype=np.float32))[None, :, None], (128, 16, 32)))
    return W


def prep_x(x, ctx, c, c_ctx, nseq_per_core, ncores):
    maps = []
    for ci in range(ncores):
        xs, cs = [], []
        for j in range(nseq_per_core):
            b = ci * nseq_per_core + j
            full = np.concatenate([ctx[b], x[b]], axis=0)
            xs.append(full.T.reshape(KT, 128, LT).transpose(1, 0, 2))
            cs.append(c[b])
        cs = cs + [c[0]] * (2 - len(cs))
        cv = np.stack([_vec(cs[0], KT), _vec(cs[1], KT), _vec(c_ctx, KT)], axis=-1)
        maps.append({"xT": np.ascontiguousarray(np.stack(xs)), "cvec": np.ascontiguousarray(cv)})
    return maps


FUSED = True
_CACHE = {}


def kernel(**inputs):
    f = lambda a: np.asarray(a, dtype=np.float32)
    W = prep_weights(inputs)
    B = inputs["x"].shape[0]
    ncores = 8
    nseq = B // ncores
    maps = prep_x(f(inputs["x"]), f(inputs["ctx"]), f(inputs["c"]), f(inputs["c_ctx"]), nseq, ncores)
    groups = [list(range(DEPTH))] if FUSED else [[l] for l in range(DEPTH)]
    out = None
    for gi, layers in enumerate(groups):
        fused = len(groups) == 1
        nc, _ = build_program(layers, nseq=nseq, fused=fused)
        in_maps = [dict(W, **m) for m in maps]
        res = run_bass_kernel_spmd(nc, in_maps, core_ids=list(range(ncores)))
        if not fused and gi < len(groups) - 1:
            for ci in range(ncores):
                maps[ci]["xT"] = np.ascontiguousarray(res.results[ci]["hout"])
        if DEPTH - 1 in layers:
            outs = []
            for ci in range(ncores):
                o = res.results[ci]["out"]
                outs.append(o.transpose(0, 3, 2, 1).reshape(nseq, LLAT, D))
            out = np.concatenate(outs, axis=0).astype(np.float32)
    return out
```
